# Optimizing a Trainium2 kernel written in Bass

```python
import math
import jax, jax.numpy as jnp
from jax import lax
import numpy as np

D_MODEL = 1024
BATCH = 4
SEQ = 8192
DEPTH = 2

N_MIXERS = 2
N_CONV = (DEPTH + 1) // 2
N_ATTN = DEPTH // 2
CONV_WIDTH = 31
N_HEADS = 8
HEAD_DIM = D_MODEL // (2 * N_HEADS)
V_DIM = 2 * HEAD_DIM
Q_BLOCK = 128
NUM_BUCKETS = 32
MAX_EXACT = NUM_BUCKETS // 2
MAX_DISTANCE = 128
D_FF = int(math.ceil(8 * D_MODEL / 3 / 256) * 256)
PLE_DIM = 256
EPS = 1e-6

kernel_name = "hybrid_conformer_diffattn_trunk"


def rms_norm(x, g):
    xf = x.astype(jnp.float32)
    y = xf * lax.rsqrt(jnp.mean(xf * xf, axis=-1, keepdims=True) + EPS)
    return (y * g.astype(jnp.float32)).astype(x.dtype)


def layer_norm(x, g, b):
    xf = x.astype(jnp.float32)
    mu = jnp.mean(xf, axis=-1, keepdims=True)
    xc = xf - mu
    var = jnp.mean(xc * xc, axis=-1, keepdims=True)
    y = xc * lax.rsqrt(var + EPS) * g.astype(jnp.float32) + b.astype(jnp.float32)
    return y.astype(x.dtype)


def conformer_conv(x, w_pw1, b_pw1, dw_w, dw_b, ln_g, ln_b, w_pw2, b_pw2):
    a = x @ w_pw1 + b_pw1
    u = a[..., :D_MODEL] * jax.nn.sigmoid(a[..., D_MODEL:])
    u = lax.conv_general_dilated(
        u, dw_w[:, None, :].astype(u.dtype), window_strides=(1,),
        padding=[(CONV_WIDTH - 1, 0)],
        dimension_numbers=("NWC", "WIO", "NWC"),
        feature_group_count=D_MODEL) + dw_b
    u = jax.nn.silu(layer_norm(u, ln_g, ln_b))
    return u @ w_pw2 + b_pw2


def t5_bucket(dist):
    n = jnp.maximum(dist, 0)
    is_small = n < MAX_EXACT
    nf = jnp.maximum(n, 1).astype(jnp.float32)
    large = MAX_EXACT + (jnp.log(nf / MAX_EXACT) / math.log(MAX_DISTANCE / MAX_EXACT)
                         * (NUM_BUCKETS - MAX_EXACT)).astype(jnp.int32)
    large = jnp.minimum(large, NUM_BUCKETS - 1)
    return jnp.where(is_small, n, large)


def diff_attention(x, w_qkv, q_g, k_g, lq1, lk1, lq2, lk2, sub_g, w_o, rel_bias, lambda_init):
    B, S, _ = x.shape
    qkv = x @ w_qkv
    q, k, v = jnp.split(qkv, 3, axis=-1)
    q = rms_norm(q.reshape(B, S, N_HEADS, 2, HEAD_DIM), q_g) * (HEAD_DIM ** -0.5)
    k = rms_norm(k.reshape(B, S, N_HEADS, 2, HEAD_DIM), k_g)
    v = v.reshape(B, S, N_HEADS, V_DIM)
    q = jnp.transpose(q, (0, 2, 3, 1, 4))
    k = jnp.transpose(k, (0, 2, 3, 1, 4))
    v = jnp.transpose(v, (0, 2, 1, 3))
    lam = (jnp.exp(jnp.sum(lq1.astype(jnp.float32) * lk1.astype(jnp.float32)))
           - jnp.exp(jnp.sum(lq2.astype(jnp.float32) * lk2.astype(jnp.float32)))
           + lambda_init)
    n_blk = S // Q_BLOCK
    qb = jnp.moveaxis(q.reshape(B, N_HEADS, 2, n_blk, Q_BLOCK, HEAD_DIM), 3, 0)
    k_pos = jnp.arange(S, dtype=jnp.int32)

    def block(args):
        q_blk, bi = args
        q_pos = bi * Q_BLOCK + jnp.arange(Q_BLOCK, dtype=jnp.int32)
        dist = q_pos[:, None] - k_pos[None, :]
        bias = jnp.transpose(rel_bias[t5_bucket(dist)], (2, 0, 1)).astype(jnp.float32)
        logits = jnp.einsum('bhcqd,bhckd->bhcqk', q_blk, k).astype(jnp.float32)
        logits = logits + bias[None, :, None]
        logits = jnp.where(dist >= 0, logits, -jnp.inf)
        probs = jax.nn.softmax(logits, axis=-1)
        w = probs[:, :, 0] - lam * probs[:, :, 1]
        return jnp.einsum('bhqk,bhkd->bhqd', w.astype(v.dtype), v)

    outs = lax.map(block, (qb, jnp.arange(n_blk, dtype=jnp.int32)))
    o = jnp.transpose(outs, (1, 0, 3, 2, 4)).reshape(B, S, N_HEADS, V_DIM)
    o = rms_norm(o, sub_g) * (1.0 - lambda_init)
    return o.reshape(B, S, N_HEADS * V_DIM) @ w_o


def swiglu(x, w_gate, w_up, w_down):
    return (jax.nn.silu(x @ w_gate) * (x @ w_up)) @ w_down


def setup_inputs(seed: int = 0) -> dict:
    key = jax.random.key(seed)
    ks = iter(jax.random.split(key, 64))
    f32 = jnp.float32

    def nrm(shape, scale):
        return jax.random.normal(next(ks), shape, f32) * scale

    def gain(shape):
        return 1.0 + nrm(shape, 0.05)

    D = D_MODEL
    return {
        "x": nrm((BATCH, SEQ, D), 1.0),
        "p": nrm((DEPTH, BATCH, SEQ, PLE_DIM), 1.0),
        "conv_norm_g": gain((N_CONV, D)),
        "conv_w_pw1": nrm((N_CONV, D, 2 * D), D ** -0.5),
        "conv_b_pw1": nrm((N_CONV, 2 * D), 0.02),
        "conv_dw_w": nrm((N_CONV, CONV_WIDTH, D), CONV_WIDTH ** -0.5),
        "conv_dw_b": nrm((N_CONV, D), 0.02),
        "conv_ln_g": gain((N_CONV, D)),
        "conv_ln_b": nrm((N_CONV, D), 0.02),
        "conv_w_pw2": nrm((N_CONV, D, D), D ** -0.5),
        "conv_b_pw2": nrm((N_CONV, D), 0.02),
        "attn_norm_g": gain((N_ATTN, D)),
        "attn_w_qkv": nrm((N_ATTN, D, 3 * D), D ** -0.5),
        "attn_q_norm_g": gain((N_ATTN, HEAD_DIM)),
        "attn_k_norm_g": gain((N_ATTN, HEAD_DIM)),
        "attn_lambda_q1": nrm((N_ATTN, HEAD_DIM), 0.1),
        "attn_lambda_k1": nrm((N_ATTN, HEAD_DIM), 0.1),
        "attn_lambda_q2": nrm((N_ATTN, HEAD_DIM), 0.1),
        "attn_lambda_k2": nrm((N_ATTN, HEAD_DIM), 0.1),
        "attn_sub_norm_g": gain((N_ATTN, V_DIM)),
        "attn_w_o": nrm((N_ATTN, D, D), D ** -0.5),
        "rel_bias": nrm((NUM_BUCKETS, N_HEADS), 0.5),
        "ffn_norm_g": gain((DEPTH, D)),
        "ffn_w_gate": nrm((DEPTH, D, D_FF), D ** -0.5),
        "ffn_w_up": nrm((DEPTH, D, D_FF), D ** -0.5),
        "ffn_w_down": nrm((DEPTH, D_FF, D), D_FF ** -0.5),
        "ple_norm_g": gain((DEPTH, D)),
        "ple_w_gate": nrm((DEPTH, D, D), D ** -0.5),
        "ple_w_proj": nrm((DEPTH, PLE_DIM, D), PLE_DIM ** -0.5),
    }


def reference(x, p, conv_norm_g, conv_w_pw1, conv_b_pw1, conv_dw_w, conv_dw_b, conv_ln_g,
              conv_ln_b, conv_w_pw2, conv_b_pw2, attn_norm_g, attn_w_qkv, attn_q_norm_g,
              attn_k_norm_g, attn_lambda_q1, attn_lambda_k1, attn_lambda_q2, attn_lambda_k2,
              attn_sub_norm_g, attn_w_o, rel_bias, ffn_norm_g, ffn_w_gate, ffn_w_up, ffn_w_down,
              ple_norm_g, ple_w_gate, ple_w_proj):
    h = x
    for i in range(DEPTH):
        j = i // N_MIXERS
        if i % N_MIXERS == 0:
            u = rms_norm(h, conv_norm_g[j])
            h = h + conformer_conv(u, conv_w_pw1[j], conv_b_pw1[j], conv_dw_w[j], conv_dw_b[j],
                                   conv_ln_g[j], conv_ln_b[j], conv_w_pw2[j], conv_b_pw2[j])
        else:
            lambda_init = 0.8 - 0.6 * math.exp(-0.3 * i)
            u = rms_norm(h, attn_norm_g[j])
            h = h + diff_attention(u, attn_w_qkv[j], attn_q_norm_g[j], attn_k_norm_g[j],
                                   attn_lambda_q1[j], attn_lambda_k1[j], attn_lambda_q2[j],
                                   attn_lambda_k2[j], attn_sub_norm_g[j], attn_w_o[j],
                                   rel_bias, lambda_init)
        u = rms_norm(h, ffn_norm_g[i])
        h = h + swiglu(u, ffn_w_gate[i], ffn_w_up[i], ffn_w_down[i])
        gate = jax.nn.sigmoid(rms_norm(h, ple_norm_g[i]) @ ple_w_gate[i])
        h = h + gate * (p[i] @ ple_w_proj[i])
    return h
```

```python
import math
from contextlib import ExitStack

import numpy as np
import ml_dtypes

import concourse.bass as bass
import concourse.mybir as mybir
from concourse.bass_utils import run_bass_kernel_spmd

F32 = mybir.dt.float32
BF16 = mybir.dt.bfloat16
AF = mybir.ActivationFunctionType
ALU = mybir.AluOpType
AX = mybir.AxisListType

D = 1024
DFF = 2816
NH = 8
T = 512
HALO = 32
TE = T + HALO
CW = 31
EPS = 1e-6
LAMBDA_INIT = 0.8 - 0.6 * math.exp(-0.3 * 1)
SLOT = 4096
NSLOT = 5
DBG_H1 = 3
DBG_SCR = False

ENGS = ("pe", "act", "dve", "pool", "sp")


class Instr:
    __slots__ = ("eng", "fn", "idx", "waits", "signal", "sigval", "dma", "dmaval", "dmainc")

    def __init__(self, eng, fn, idx, dma):
        self.eng = eng
        self.fn = fn
        self.idx = idx
        self.waits = []
        self.signal = False
        self.sigval = 0
        self.dma = dma
        self.dmaval = 0
        self.dmainc = 16


class Sched:
    def __init__(self, nc):
        self.nc = nc
        self.streams = {e: [] for e in ENGS}
        self.lastw = {}
        self.readers = {}
        self.dma_counts = {}
        self.waited = {e: {} for e in ENGS}

    def add(self, eng, fn, reads=(), writes=(), dma=None, dma_inc=16):
        lst = self.streams[eng]
        ins = Instr(eng, fn, len(lst), dma)
        if dma is not None:
            c = self.dma_counts.get(dma, 0) + 1
            self.dma_counts[dma] = c
            ins.dmaval = dma_inc * c
            ins.dmainc = dma_inc
        deps = {}
        for r in reads:
            w = self.lastw.get(r)
            if w is not None:
                deps[w] = True
        for r in writes:
            w = self.lastw.get(r)
            if w is not None:
                deps[w] = True
            for rd in self.readers.get(r, ()):
                if rd not in deps:
                    deps[rd] = False
        wd = self.waited[eng]
        for d, strong in deps.items():
            if d is ins:
                continue
            if d.dma is not None:
                if dma is not None and d.dma == dma and d.eng == eng:
                    continue
                if wd.get(d.dma, 0) >= d.dmaval:
                    continue
                wd[d.dma] = d.dmaval
                ins.waits.append(d)
            else:
                if d.eng == eng and (eng == "pe" or not strong):
                    continue
                if wd.get(d.eng, -1) >= d.idx:
                    continue
                wd[d.eng] = d.idx
                d.signal = True
                ins.waits.append(d)
        for r in writes:
            self.lastw[r] = ins
            self.readers[r] = []
        for r in reads:
            self.readers.setdefault(r, []).append(ins)
        lst.append(ins)
        return ins

    def barrier(self):
        lasts = []
        for e in ENGS:
            for cand in reversed(self.streams[e]):
                if cand.dma is None and cand.fn is not None:
                    lasts.append(cand)
                    break
        dmas = {}
        for e in ENGS:
            for ins in self.streams[e]:
                if ins.dma is not None:
                    dmas[ins.dma] = ins
        for e in ENGS:
            ins = Instr(e, None, len(self.streams[e]), None)
            wd = self.waited[e]
            for d in lasts:
                if d.eng == e:
                    continue
                if wd.get(d.eng, -1) >= d.idx:
                    continue
                wd[d.eng] = d.idx
                d.signal = True
                ins.waits.append(d)
            for key, d in dmas.items():
                if wd.get(key, 0) >= d.dmaval:
                    continue
                wd[key] = d.dmaval
                ins.waits.append(d)
            self.streams[e].append(ins)

    def emit(self):
        nc = self.nc
        for e in ENGS:
            c = 0
            for ins in self.streams[e]:
                if ins.dma is None and ins.signal:
                    c += 1
                    ins.sigval = c
        with ExitStack() as es:
            esem = {e: es.enter_context(nc.semaphore("s_" + e)) for e in ENGS}
            dsem = {k: es.enter_context(nc.semaphore("d_" + str(k))) for k in self.dma_counts}
            block = es.enter_context(nc.Block())

            def run(engobj, e):
                for ins in self.streams[e]:
                    for d in ins.waits:
                        if d.dma is not None:
                            engobj.wait_ge(dsem[d.dma], d.dmaval)
                        else:
                            engobj.wait_ge(esem[d.eng], d.sigval)
                    if ins.fn is None:
                        continue
                    bi = ins.fn(engobj)
                    if ins.dma is not None:
                        bi.then_inc(dsem[ins.dma], ins.dmainc)
                    elif ins.signal:
                        bi.then_inc(esem[e], 1)

            @block.tensor
            def _(eng):
                run(eng, "pe")

            @block.scalar
            def _(eng):
                run(eng, "act")

            @block.vector
            def _(eng):
                run(eng, "dve")

            @block.gpsimd
            def _(eng):
                run(eng, "pool")

            @block.sync
            def _(eng):
                run(eng, "sp")


class Arena:
    def __init__(self, t, n):
        self.t = t
        self.n = n
        self.off = 0

    def alloc(self, n):
        nr = (n + 15) // 16 * 16
        assert self.off + nr <= self.n, ("arena overflow", self.off, nr, self.n)
        ap = self.t[:, self.off:self.off + n]
        self.off += nr
        return ap


def _cols(v):
    v = np.asarray(v, np.float32).reshape(-1, 128)
    return np.ascontiguousarray(v.T)


def t5_bucket_np(dist):
    n = np.maximum(dist, 0)
    is_small = n < 16
    nf = np.maximum(n, 1).astype(np.float32)
    large = 16 + (np.log(nf / np.float32(16)) / np.float32(math.log(128 / 16)) * np.float32(16)).astype(np.int32)
    large = np.minimum(large, 31)
    return np.where(is_small, n, large)


class CLayout:
    def __init__(self):
        self.off = {}
        self.n = 0

    def add(self, name, w):
        self.off[name] = (self.n, w)
        self.n += w


def const_layout(NT):
    L = CLayout()
    for name, w in [
        ("b_pw1", 16), ("dw_w", 8 * CW), ("dw_b", 8), ("ln_g", 8), ("ln_b", 8), ("b_pw2", 8),
        ("conv_ng", 8), ("ffn_ng0", 8), ("ple_ng0", 8), ("attn_ng", 8), ("ffn_ng1", 8), ("ple_ng1", 8),
        ("q_g", 1), ("k_g", 1), ("sub_g", 1), ("eps", 1), ("eps128", 1),
        ("halo_valid", NT), ("lq1", 64), ("lk1", 64), ("lq2", 64), ("lk2", 64),
        ("relb", 8), ("ohd", 256), ("sel31", 128), ("ident", 128), ("antiI", 128), ("blk64", 128),
        ("bgval", 36), ("cnear", 36), ("cdiag", 36),
    ]:
        L.add(name, w)
    return L


def build_consts(inp, NT, r):
    L = const_layout(NT)
    C = np.zeros((128, L.n), np.float32)

    def put(name, arr):
        o, w = L.off[name]
        arr = np.asarray(arr, np.float32)
        C[:arr.shape[0], o:o + w] = arr.reshape(arr.shape[0], w)

    put("b_pw1", _cols(inp["conv_b_pw1"][0]))
    dw = np.asarray(inp["conv_dw_w"][0], np.float32)
    dwl = np.zeros((128, 8, CW), np.float32)
    for i in range(8):
        dwl[:, i, :] = dw[:, i * 128:(i + 1) * 128].T
    put("dw_w", dwl.reshape(128, 8 * CW))
    put("dw_b", _cols(inp["conv_dw_b"][0]))
    put("ln_g", _cols(inp["conv_ln_g"][0]))
    put("ln_b", _cols(inp["conv_ln_b"][0]))
    put("b_pw2", _cols(inp["conv_b_pw2"][0]))
    put("conv_ng", _cols(inp["conv_norm_g"][0]))
    put("ffn_ng0", _cols(inp["ffn_norm_g"][0]))
    put("ple_ng0", _cols(inp["ple_norm_g"][0]))
    put("attn_ng", _cols(inp["attn_norm_g"][0]))
    put("ffn_ng1", _cols(inp["ffn_norm_g"][1]))
    put("ple_ng1", _cols(inp["ple_norm_g"][1]))
    put("q_g", np.tile(np.asarray(inp["attn_q_norm_g"][0], np.float32), 2).reshape(128, 1))
    put("k_g", np.tile(np.asarray(inp["attn_k_norm_g"][0], np.float32), 2).reshape(128, 1))
    put("sub_g", np.asarray(inp["attn_sub_norm_g"][0], np.float32).reshape(128, 1))
    put("eps", np.full((128, 1), EPS, np.float32))
    put("eps128", np.full((128, 1), 128 * EPS, np.float32))
    hv = np.ones((128, NT), np.float32)
    if r == 0:
        hv[:, 0] = 0.0
    put("halo_valid", hv)
    for nm, key in [("lq1", "attn_lambda_q1"), ("lk1", "attn_lambda_k1"),
                    ("lq2", "attn_lambda_q2"), ("lk2", "attn_lambda_k2")]:
        put(nm, np.tile(np.asarray(inp[key][0], np.float32)[None, :], (128, 1)))
    put("relb", np.asarray(inp["rel_bias"], np.float32))
    bk = t5_bucket_np(np.arange(256, dtype=np.int32))
    oh = np.zeros((32, 256), np.float32)
    oh[bk, np.arange(256)] = 1.0
    put("ohd", oh)
    s31 = np.zeros((32, 128), np.float32)
    s31[31, :] = 1.0
    put("sel31", s31)
    put("ident", np.eye(128, dtype=np.float32))
    put("antiI", np.eye(128, dtype=np.float32)[::-1])
    b64 = np.zeros((128, 128), np.float32)
    b64[:64, :64] = 1.0 / 64
    b64[64:, 64:] = 1.0 / 64
    put("blk64", b64)
    bg = np.zeros((9, 4), np.float32)
    cn = np.zeros((9, 4), np.float32)
    cd = np.zeros((9, 4), np.float32)
    for s in range(9):
        for qs in range(4):
            delta = 4 * r + qs + 1 - s
            if delta >= 2:
                bg[s, qs] = 1.0
            elif delta == 1:
                cn[s, qs] = 1.0
            elif delta == 0:
                cd[s, qs] = 1.0
    put("bgval", np.tile(bg.reshape(1, 36), (128, 1)))
    put("cnear", np.tile(cn.reshape(1, 36), (128, 1)))
    put("cdiag", np.tile(cd.reshape(1, 36), (128, 1)))
    return C


class WSpec:
    def __init__(self, name, src, k0, kc, c0, ncols, wide, scale, gcols):
        self.name = name
        self.src = src
        self.k0 = k0
        self.kc = kc
        self.c0 = c0
        self.ncols = ncols
        self.wide = wide
        self.scale = scale
        self.gcols = gcols
        self.ngroups = (ncols + gcols - 1) // gcols
        self.scr = None

    def gsize(self, g):
        return min(self.gcols, self.ncols - g * self.gcols)


class Builder:
    def __init__(self, NT, prog):
        self.NT = NT
        self.prog = prog
        self.nc = bass.Bass("TRN2", target_bir_lowering=False)
        self.S = Sched(self.nc)
        self.L = const_layout(NT)
        self.bank_rr = 0
        self.sc_rr = 0

    def cc(self, name, j=0, w=1, p0=0, p1=128):
        o, _ = self.L.off[name]
        return self.cst[p0:p1, o + j:o + j + w]

    def next_bank(self):
        b = self.banks[self.bank_rr % len(self.banks)]
        self.bank_rr += 1
        return b

    def next_sc(self):
        i = self.sc_rr % len(self.scs)
        self.sc_rr += 1
        return self.scs[i], ("sc", i)

    def dram_in(self, name, shape, dt=F32):
        return self.nc.dram_tensor(name, list(shape), dt, kind="ExternalInput").ap()

    def dram_out(self, name, shape, dt=F32):
        return self.nc.dram_tensor(name, list(shape), dt, kind="ExternalOutput").ap()

    def preconvert(self, specs, st32, stb):
        S = self.S
        n = 0
        for ws in specs:
            ws.scr = self.nc.dram_tensor("scr_" + ws.name, [ws.ngroups, 128, SLOT], BF16,
                                         kind="ExternalOutput" if DBG_SCR else "Internal").ap()
            srcv = ws.src[ws.k0:ws.k0 + ws.kc * 128, :].rearrange("(k p) n -> p k n", p=128)
            for g in range(ws.ngroups):
                gc = ws.gsize(g)
                c0 = ws.c0 + g * ws.gcols
                b = n % 2
                n += 1
                s32 = st32[b][:, 0:ws.kc * gc]
                sb = stb[b][:, 0:ws.kc * gc]
                S.add("sp", lambda e, s32=s32, srcv=srcv, c0=c0, gc=gc: e.dma_start(
                    out=s32.rearrange("p (k n) -> p k n", n=gc), in_=srcv[:, :, c0:c0 + gc]),
                    writes=[("st32", b)], dma=("st32", b))
                for k in range(ws.kc):
                    eng = "dve" if (k % 2 == 0) else "pool"
                    if ws.wide:
                        o_ap = sb[:, k * gc:(k + 1) * gc]
                        i_ap = s32[:, k * gc:(k + 1) * gc]
                    else:
                        npan = gc // 128
                        o_ap = sb.rearrange("p (n k c) -> p n k c", k=ws.kc, c=128)[:, :, k, :]
                        i_ap = s32.rearrange("p (k n c) -> p k n c", n=npan, c=128)[:, k, :, :]
                    if ws.scale is None:
                        S.add(eng, lambda e, o_ap=o_ap, i_ap=i_ap: e.tensor_copy(o_ap, i_ap),
                              reads=[("st32", b)], writes=[("stb", b, k)])
                    else:
                        sc = self.cc(ws.scale, k if ws.scale != "wo_scale" else 0)
                        S.add(eng, lambda e, o_ap=o_ap, i_ap=i_ap, sc=sc: e.tensor_scalar_mul(o_ap, i_ap, sc),
                              reads=[("st32", b), "cst"], writes=[("stb", b, k)])
                S.add("act", lambda e, ws=ws, g=g, sb=sb, gc=gc: e.dma_start(
                    out=ws.scr[g, :, 0:ws.kc * gc], in_=sb),
                    reads=[("stb", b, k) for k in range(ws.kc)], writes=[("scr", ws.name, g)],
                    dma=("stb", b))

    def ws_init(self, plan):
        self.plan = plan
        self.ws_cur = 0
        self.ws_emitted = 0

    def ws_next(self, ws, g):
        S = self.S
        assert self.plan[self.ws_cur] == (ws.name, g), (self.plan[self.ws_cur], ws.name, g)
        hi = min(len(self.plan), self.ws_cur + NSLOT - 1)
        while self.ws_emitted < hi:
            m = self.ws_emitted
            nm, gg = self.plan[m]
            w2 = self.wspecs[nm]
            sl = m % NSLOT
            ne = w2.kc * w2.gsize(gg)
            S.add("sp", lambda e, sl=sl, w2=w2, gg=gg, ne=ne: e.dma_start(
                out=self.wslots[sl][:, 0:ne], in_=w2.scr[gg, :, 0:ne]),
                reads=[("scr", nm, gg)], writes=[("wslot", sl)], dma=("wslot", sl))
            self.ws_emitted += 1
        sl = self.ws_cur % NSLOT
        self.ws_cur += 1
        return self.wslots[sl], ("wslot", sl)

    def mm_chain(self, bank, ncols, pairs, reads, col0=0):
        out = self.ps[bank][:, col0:col0 + ncols]
        n = len(pairs)

        def fn(e):
            bi = None
            for i, (l, r) in enumerate(pairs):
                bi = e.matmul(out, l, r, start=(i == 0), stop=(i == n - 1))
            return bi
        self.S.add("pe", fn, reads=reads, writes=[("ps", bank)])

    def rms_stats(self, src, ncols_list, sqb, rstd, src_res):
        S = self.S
        W = sqb.shape[1] // 8
        S.add("act", lambda e: e.activation(sqb, src, AF.Square), reads=[src_res], writes=["sqb"])
        for (c0, n) in ncols_list:
            bank = self.next_bank()
            self.mm_chain(bank, n, [(self.onesb, sqb[:, k * W + c0:k * W + c0 + n]) for k in range(8)],
                          reads=["sqb", "constb"])
            o = rstd[:, c0:c0 + n]
            S.add("act", lambda e, o=o, bank=bank, n=n: e.activation(
                o, self.ps[bank][:, 0:n], AF.Sqrt, bias=self.cc("eps"), scale=1.0),
                reads=[("ps", bank)], writes=[("rstd", c0)])
            S.add("dve", lambda e, o=o: e.reciprocal(o, o), reads=[("rstd", c0)], writes=[("rstd", c0)])

    def normalize(self, src, hn, rstd, W, c0, n, src_res, rres):
        S = self.S
        for k in range(8):
            eng = "dve" if k % 2 == 0 else "pool"
            o = hn[:, k * W + c0:k * W + c0 + n]
            i = src[:, k * W + c0:k * W + c0 + n]
            rr = rstd[:, c0:c0 + n]
            S.add(eng, lambda e, o=o, i=i, rr=rr: e.tensor_tensor(o, i, rr, ALU.mult),
                  reads=[src_res, rres], writes=[("hn", k, c0)])

    def ffn_ple(self, layer, hfull, HW, hoff, pTb, after_ffn=None):
        S = self.S
        W = self.wspecs
        sfx = str(layer)
        hres = "h"

        def hmain(k):
            return hfull[:, k * HW + hoff:k * HW + hoff + T]
        sqb = self.sqb[:, 0:8 * HW]
        self.rms_stats(hfull, [(hoff, T)], sqb, self.rstd, hres)
        self.normalize(hfull, self.hn, self.rstd, HW, hoff, T, hres, ("rstd", hoff))
        hn_res = [("hn", k, hoff) for k in range(8)]

        def hnk(k):
            return self.hn[:, k * HW + hoff:k * HW + hoff + T]
        wg, wu = W["wg" + sfx], W["wu" + sfx]
        for g in range(wg.ngroups):
            sg, rg = self.ws_next(wg, g)
            su, ru = self.ws_next(wu, g)
            npan = wg.gsize(g) // 128
            for pi in range(npan):
                ci = g * 4 + pi
                bg_ = self.next_bank()
                bu_ = self.next_bank()
                self.mm_chain(bg_, T, [(sg[:, (pi * 8 + k) * 128:(pi * 8 + k + 1) * 128], hnk(k)) for k in range(8)],
                              reads=[rg] + hn_res)
                self.mm_chain(bu_, T, [(su[:, (pi * 8 + k) * 128:(pi * 8 + k + 1) * 128], hnk(k)) for k in range(8)],
                              reads=[ru] + hn_res)
                sc, scr = self.next_sc()
                S.add("act", lambda e, sc=sc, bg_=bg_: e.activation(sc[:, 0:T], self.ps[bg_][:, 0:T], AF.Silu),
                      reads=[("ps", bg_)], writes=[scr])
                a = self.abuf[:, ci * T:(ci + 1) * T]
                S.add("dve", lambda e, a=a, bu_=bu_, sc=sc: e.tensor_tensor(a, self.ps[bu_][:, 0:T], sc[:, 0:T], ALU.mult),
                      reads=[("ps", bu_), scr], writes=[("a", ci)])
        wd = W["wd" + sfx]
        a_res = [("a", ci) for ci in range(22)]
        for g in range(8):
            sd, rd = self.ws_next(wd, g)
            bank = self.next_bank()
            self.mm_chain(bank, T, [(sd[:, k * 128:(k + 1) * 128], self.abuf[:, k * T:(k + 1) * T]) for k in range(22)],
                          reads=[rd] + a_res)
            hk = hmain(g)
            S.add("dve", lambda e, hk=hk, bank=bank: e.tensor_tensor(hk, self.ps[bank][:, 0:T], hk, ALU.add),
                  reads=[("ps", bank), hres], writes=[hres])
        if after_ffn is not None:
            after_ffn()
        self.rms_stats(hfull, [(hoff, T)], sqb, self.rstd, hres)
        self.normalize(hfull, self.hn, self.rstd, HW, hoff, T, hres, ("rstd", hoff))
        pg, pp = W["pg" + sfx], W["pp" + sfx]
        spp, rpp = None, None
        for g in range(2):
            sgt, rgt = self.ws_next(pg, g)
            if g == 0:
                spp, rpp = self.ws_next(pp, 0)
            for pi in range(4):
                ci = g * 4 + pi
                bg_ = self.next_bank()
                bp_ = self.next_bank()
                self.mm_chain(bg_, T, [(sgt[:, (pi * 8 + k) * 128:(pi * 8 + k + 1) * 128], hnk(k)) for k in range(8)],
                              reads=[rgt] + hn_res)
                self.mm_chain(bp_, T, [(spp[:, (ci * 2 + k) * 128:(ci * 2 + k + 1) * 128], pTb[:, k * T:(k + 1) * T]) for k in range(2)],
                              reads=[rpp, "pTb"])
                sc, scr = self.next_sc()
                S.add("act", lambda e, sc=sc, bg_=bg_: e.activation(sc[:, 0:T], self.ps[bg_][:, 0:T], AF.Sigmoid),
                      reads=[("ps", bg_)], writes=[scr])
                S.add("dve", lambda e, sc=sc, bp_=bp_: e.tensor_tensor(sc[:, 0:T], self.ps[bp_][:, 0:T], sc[:, 0:T], ALU.mult),
                      reads=[("ps", bp_), scr], writes=[scr])
                hk = hmain(ci)
                S.add("pool", lambda e, hk=hk, sc=sc: e.tensor_tensor(hk, hk, sc[:, 0:T], ALU.add),
                      reads=[scr, hres], writes=[hres])

    def ffn_plan(self, layer):
        sfx = str(layer)
        pl = []
        for g in range(6):
            pl += [("wg" + sfx, g), ("wu" + sfx, g)]
        pl += [("wd" + sfx, g) for g in range(8)]
        pl += [("pg" + sfx, 0), ("pp" + sfx, 0), ("pg" + sfx, 1)]
        return pl

    def common_setup(self, es, nf32, nbf):
        nc = self.nc
        self.fa = Arena(es.enter_context(nc.sbuf_tensor("fa", [128, nf32], F32)), nf32)
        self.ba = Arena(es.enter_context(nc.sbuf_tensor("ba", [128, nbf], BF16)), nbf)
        self.cst = self.fa.alloc(self.L.n)[:, 0:self.L.n]
        self.constb = self.ba.alloc(128 * 4)
        self.onesb = self.constb[:, 0:128]
        self.identb = self.constb[:, 128:256]
        self.blk64b = self.constb[:, 256:384]
        S = self.S
        S.add("sp", lambda e: e.dma_start(out=self.cst, in_=self.cst_in), writes=["cst"], dma="cst")
        S.add("pool", lambda e: e.memset(self.onesb, 1.0 / D), writes=["constb"])
        S.add("dve", lambda e: e.tensor_copy(self.identb, self.cc("ident", 0, 128)), reads=["cst"], writes=["constb"])
        S.add("dve", lambda e: e.tensor_copy(self.blk64b, self.cc("blk64", 0, 128)), reads=["cst"], writes=["constb"])

    def load_pT(self, pT_in, j, p32, pTb):
        S = self.S
        S.add("sp", lambda e: e.dma_start(out=p32.rearrange("p (k t) -> p k t", t=T),
                                          in_=pT_in[j].rearrange("k p t -> p k t")),
              writes=["p32"], dma="p32")
        S.add("pool", lambda e: e.tensor_copy(pTb, p32), reads=["p32"], writes=["pTb"])


def prog1_io(B, NT, fused):
    io = {}
    io["xh"] = B.dram_in("xh", [NT, 8, 128, TE])
    io["pT"] = B.dram_in("pT0" if fused else "pT", [NT, 2, 128, T])
    sfx = "0" if fused else ""
    io["w_pw1"] = B.dram_in("conv_w_pw1", [D, 2 * D])
    io["w_pw2"] = B.dram_in("conv_w_pw2", [D, D])
    io["w_gate"] = B.dram_in("ffn_w_gate" + sfx, [D, DFF])
    io["w_up"] = B.dram_in("ffn_w_up" + sfx, [D, DFF])
    io["w_down"] = B.dram_in("ffn_w_down" + sfx, [DFF, D])
    io["w_pg"] = B.dram_in("ple_w_gate" + sfx, [D, D])
    io["w_pp"] = B.dram_in("ple_w_proj" + sfx, [256, D])
    io["w_qkv"] = B.dram_in("attn_w_qkv", [D, 3 * D])
    if fused:
        nc = B.nc
        io["h1"] = nc.dram_tensor("h1", [NT, 8, 128, T], F32).ap()
        io["qT"] = nc.dram_tensor("qT", [NT, 8, 128, T], BF16).ap()
        io["kT_list"] = [nc.dram_tensor(f"kT{j}", [8 * 128, T], BF16).ap() for j in range(NT)]
        io["v_list"] = [nc.dram_tensor(f"v{j}", [4 * 128, NH * 129], BF16).ap() for j in range(NT)]
        io["kT"] = [a.rearrange("(h p) t -> h p t", p=128) for a in io["kT_list"]]
        io["v"] = [a.rearrange("(b p) c -> b p c", p=128) for a in io["v_list"]]
    else:
        io["h1"] = B.dram_out("h1", [NT, 8, 128, T])
        io["qT"] = B.dram_out("qT", [NT, 8, 128, T], BF16)
        io["kT"] = B.dram_out("kT", [NT, 8, 128, T], BF16)
        io["v"] = B.dram_out("v", [NT, 4, 128, NH * 129], BF16)
    return io


def prog1_specs(io):
    return [
        WSpec("w1a", io["w_pw1"], 0, 8, 0, D, False, "conv_ng", 512),
        WSpec("w1b", io["w_pw1"], 0, 8, D, D, False, "conv_ng", 512),
        WSpec("w2", io["w_pw2"], 0, 8, 0, D, False, None, 512),
        WSpec("wg0", io["w_gate"], 0, 8, 0, DFF, False, "ffn_ng0", 512),
        WSpec("wu0", io["w_up"], 0, 8, 0, DFF, False, "ffn_ng0", 512),
        WSpec("wd0", io["w_down"], 0, 22, 0, D, False, None, 128),
        WSpec("pg0", io["w_pg"], 0, 8, 0, D, False, "ple_ng0", 512),
        WSpec("pp0", io["w_pp"], 0, 2, 0, D, False, None, 1024),
        WSpec("wqk", io["w_qkv"], 0, 8, 0, 2 * D, False, "attn_ng", 512),
        WSpec("wv", io["w_qkv"], 0, 8, 2 * D, D, True, "attn_ng", 512),
    ]


def prog1_plan(B):
    plan_tile = [("w1a", 0), ("w1b", 0), ("w1a", 1), ("w1b", 1), ("w2", 0), ("w2", 1)]
    plan_tile += B.ffn_plan(0)
    plan_tile += [("wqk", g) for g in range(4)] + [("wv", 0), ("wv", 1)]
    return plan_tile


P1_F32 = 64 + 8 * TE + 8 * T + 9 * TE + 2 * T + 64
P1_BF = NSLOT * SLOT + 2 * CW * 128 + 3 * 8 * TE + 3 * 8 * T + 2 * T + 4 * 1040 + 64


def build_prog1(NT):
    B = Builder(NT, 1)
    nc, S = B.nc, B.S
    B.cst_in = B.dram_in("cst", [128, B.L.n])
    io = prog1_io(B, NT, False)
    specs = prog1_specs(io)
    B.wspecs = {w.name: w for w in specs}
    B.ws_init(prog1_plan(B) * NT)
    with ExitStack() as es:
        B.common_setup(es, nf32=B.L.n + 16 + P1_F32, nbf=512 + P1_BF)
        B.ps = [es.enter_context(nc.psum_tensor(f"ps{i}", [128, 512], F32)) for i in range(8)]
        B.banks = list(range(8))
        fa, ba = B.fa, B.ba
        mark_f, mark_b = fa.off, ba.off
        st32 = [fa.alloc(4096), fa.alloc(4096)]
        stb = [ba.alloc(4096), ba.alloc(4096)]
        B.preconvert(specs, st32, stb)
        fa.off, ba.off = mark_f, mark_b
        prog1_body(B, io, NT)
        S.barrier()
        S.emit()
    return nc


def prog1_body(B, io, NT):
    nc, S = B.nc, B.S
    fa, ba = B.fa, B.ba
    xh, pT_in = io["xh"], io["pT"]
    h1_out, q_out, k_out, v_out = io["h1"], io["qT"], io["kT"], io["v"]
    if True:
        B.wslots = [ba.alloc(SLOT) for _ in range(NSLOT)]
        dbuf = [ba.alloc(CW * 128) for _ in range(2)]
        h_ext = fa.alloc(8 * TE)
        y32 = fa.alloc(8 * T)
        B.scs = [fa.alloc(TE) for _ in range(4)]
        B.rstd = fa.alloc(TE)
        mean_sb = fa.alloc(TE)
        rstd2 = fa.alloc(TE)
        nmr = fa.alloc(TE)
        tmpv = fa.alloc(TE)
        p32 = fa.alloc(2 * T)
        B.sqb = ba.alloc(8 * TE)
        B.hn = ba.alloc(8 * TE)
        u_ext = ba.alloc(8 * TE)
        B.abuf = ba.alloc(24 * T)
        ybf = B.abuf[:, 0:8 * T]
        ysq = B.abuf[:, 8 * T:16 * T]
        zb = B.abuf[:, 16 * T:24 * T]
        pTb = ba.alloc(2 * T)
        vt = ba.alloc(4 * NH * 129)
        hvt = fa.alloc(16)
        S.barrier()
        S.add("pool", lambda e: e.memset(vt, 1.0), writes=["vt"])
        gq8 = hvt[:, 0:1]
        S.add("dve", lambda e: e.tensor_scalar_mul(gq8, B.cc("q_g"), 0.125), reads=["cst"], writes=["gq8"])

        def store_h1(j):
            S.add("pool", lambda e, j=j: e.dma_start(
                out=h1_out[j].rearrange("k p t -> p k t"),
                in_=h_ext.rearrange("p (k t) -> p k t", t=TE)[:, :, HALO:TE]),
                reads=["h"], writes=[("h1o", j)], dma="h1o")

        def dbg_dump(j, srcfn, reads):
            for k in range(8):
                S.add("dve", lambda e, k=k: e.tensor_copy(h_ext[:, k * TE + HALO:k * TE + TE], srcfn(k)),
                      reads=list(reads) + ["h"], writes=["h"])
            store_h1(j)

        for j in range(NT):
            S.add("sp", lambda e, j=j: e.dma_start(out=h_ext.rearrange("p (k t) -> p k t", t=TE),
                                                   in_=xh[j].rearrange("k p t -> p k t")),
                  writes=["h"], dma="h")
            B.load_pT(pT_in, j, p32, pTb)
            if DBG_H1 == 0:
                store_h1(j)
            B.rms_stats(h_ext, [(HALO, T), (0, HALO)], B.sqb, B.rstd, "h")
            B.normalize(h_ext, B.hn, B.rstd, TE, HALO, T, "h", ("rstd", HALO))
            B.normalize(h_ext, B.hn, B.rstd, TE, 0, HALO, "h", ("rstd", 0))
            hn_main = [("hn", k, HALO) for k in range(8)]
            hn_halo = [("hn", k, 0) for k in range(8)]
            if DBG_H1 == 10:
                dbg_dump(j, lambda k: B.hn[:, k * TE + HALO:k * TE + TE], hn_main)
            for g in range(2):
                sa, ra = B.ws_next(B.wspecs["w1a"], g)
                sb_, rb = B.ws_next(B.wspecs["w1b"], g)
                for pi in range(4):
                    ci = g * 4 + pi
                    for (c0, n, hres_) in ((HALO, T, hn_main), (0, HALO, hn_halo)):
                        ba_ = B.next_bank()
                        bb_ = B.next_bank()
                        B.mm_chain(ba_, n, [(sa[:, (pi * 8 + k) * 128:(pi * 8 + k + 1) * 128],
                                             B.hn[:, k * TE + c0:k * TE + c0 + n]) for k in range(8)],
                                   reads=[ra] + hres_)
                        B.mm_chain(bb_, n, [(sb_[:, (pi * 8 + k) * 128:(pi * 8 + k + 1) * 128],
                                             B.hn[:, k * TE + c0:k * TE + c0 + n]) for k in range(8)],
                                   reads=[rb] + hres_)
                        sc, scr = B.next_sc()
                        S.add("act", lambda e, sc=sc, bb_=bb_, n=n, ci=ci: e.activation(
                            sc[:, 0:n], B.ps[bb_][:, 0:n], AF.Sigmoid, bias=B.cc("b_pw1", 8 + ci), scale=1.0),
                            reads=[("ps", bb_)], writes=[scr])
                        uo = u_ext[:, ci * TE + c0:ci * TE + c0 + n]
                        if DBG_H1 == 16:
                            S.add("act", lambda e, uo=uo, ba_=ba_, n=n: e.activation(uo, B.ps[ba_][:, 0:n], AF.Copy),
                                  reads=[("ps", ba_), scr], writes=[("u", ci, c0)])
                            continue
                        if DBG_H1 == 17:
                            S.add("act", lambda e, uo=uo, sa=sa, n=n, pi=pi: e.activation(uo, sa[:, pi * 1024:pi * 1024 + n], AF.Copy),
                                  reads=[ra, ("ps", ba_), scr], writes=[("u", ci, c0)])
                            continue
                        S.add("dve", lambda e, uo=uo, ba_=ba_, sc=sc, n=n, ci=ci: e.scalar_tensor_tensor(
                            uo, B.ps[ba_][:, 0:n], B.cc("b_pw1", ci), sc[:, 0:n], ALU.add, ALU.mult),
                            reads=[("ps", ba_), scr], writes=[("u", ci, c0)])
                        if c0 == 0:
                            S.add("pool", lambda e, uo=uo, j=j: e.tensor_scalar_mul(uo, uo, B.cc("halo_valid", j)),
                                  reads=[("u", ci, 0)], writes=[("u", ci, 0)])
            if DBG_H1 in (11, 16, 17):
                dbg_dump(j, lambda k: u_ext[:, k * TE + HALO:k * TE + TE], [("u", k, HALO) for k in range(8)])
            for ci in range(8):
                db = ci % 2
                for k in range(CW):
                    o = dbuf[db][:, k * 128:(k + 1) * 128]
                    sc = B.cc("dw_w", ci * CW + k)
                    eng = "dve" if (k % 2 == 0) else "pool"
                    S.add(eng, lambda e, o=o, sc=sc: e.tensor_scalar_mul(o, B.cc("ident", 0, 128), sc),
                          reads=["cst"], writes=[("dmat", db, k)])
                bank = B.next_bank()
                B.mm_chain(bank, T, [(dbuf[db][:, k * 128:(k + 1) * 128],
                                      u_ext[:, ci * TE + k + 2:ci * TE + k + 2 + T]) for k in range(CW)],
                           reads=[("dmat", db, k) for k in range(CW)] + [("u", ci, 0), ("u", ci, HALO)])
                yo = y32[:, ci * T:(ci + 1) * T]
                S.add("act", lambda e, yo=yo, bank=bank, ci=ci: e.activation(
                    yo, B.ps[bank][:, 0:T], AF.Identity, bias=B.cc("dw_b", ci), scale=1.0),
                    reads=[("ps", bank)], writes=[("y", ci)])
                S.add("act", lambda e, bank=bank, ci=ci: e.activation(
                    ysq[:, ci * T:(ci + 1) * T], B.ps[bank][:, 0:T], AF.Square, bias=B.cc("dw_b", ci), scale=1.0),
                    reads=[("ps", bank)], writes=[("ysq", ci)])
                S.add("pool", lambda e, yo=yo, ci=ci: e.tensor_copy(ybf[:, ci * T:(ci + 1) * T], yo),
                      reads=[("y", ci)], writes=[("ybf", ci)])
            if DBG_H1 == 12:
                dbg_dump(j, lambda k: y32[:, k * T:(k + 1) * T], [("y", k) for k in range(8)])
            bm = B.next_bank()
            bq = B.next_bank()
            B.mm_chain(bm, T, [(B.onesb, ybf[:, k * T:(k + 1) * T]) for k in range(8)],
                       reads=["constb"] + [("ybf", k) for k in range(8)])
            B.mm_chain(bq, T, [(B.onesb, ysq[:, k * T:(k + 1) * T]) for k in range(8)],
                       reads=["constb"] + [("ysq", k) for k in range(8)])
            S.add("act", lambda e, bm=bm: e.activation(mean_sb[:, 0:T], B.ps[bm][:, 0:T], AF.Copy),
                  reads=[("ps", bm)], writes=["mean"])
            S.add("dve", lambda e: e.tensor_tensor(tmpv[:, 0:T], mean_sb[:, 0:T], mean_sb[:, 0:T], ALU.mult),
                  reads=["mean"], writes=["tmpv"])
            S.add("dve", lambda e, bq=bq: e.tensor_tensor(tmpv[:, 0:T], B.ps[bq][:, 0:T], tmpv[:, 0:T], ALU.subtract),
                  reads=[("ps", bq), "tmpv"], writes=["tmpv"])
            S.add("act", lambda e: e.activation(rstd2[:, 0:T], tmpv[:, 0:T], AF.Sqrt, bias=B.cc("eps"), scale=1.0),
                  reads=["tmpv"], writes=["rstd2"])
            S.add("dve", lambda e: e.reciprocal(rstd2[:, 0:T], rstd2[:, 0:T]), reads=["rstd2"], writes=["rstd2"])
            S.add("dve", lambda e: e.scalar_tensor_tensor(nmr[:, 0:T], mean_sb[:, 0:T], -1.0, rstd2[:, 0:T],
                                                          ALU.mult, ALU.mult),
                  reads=["mean", "rstd2"], writes=["nmr"])
            for ci in range(8):
                yo = y32[:, ci * T:(ci + 1) * T]
                S.add("dve", lambda e, yo=yo: e.tensor_tensor(yo, yo, rstd2[:, 0:T], ALU.mult),
                      reads=[("y", ci), "rstd2"], writes=[("y", ci)])
                S.add("pool", lambda e, yo=yo: e.tensor_tensor(yo, yo, nmr[:, 0:T], ALU.add),
                      reads=[("y", ci), "nmr"], writes=[("y", ci)])
                S.add("act", lambda e, yo=yo, ci=ci: e.activation(
                    zb[:, ci * T:(ci + 1) * T], yo, AF.Silu, bias=B.cc("ln_b", ci), scale=B.cc("ln_g", ci)),
                    reads=[("y", ci)], writes=[("z", ci)])
            if DBG_H1 == 13:
                dbg_dump(j, lambda k: zb[:, k * T:(k + 1) * T], [("z", k) for k in range(8)])
            z_res = [("z", k) for k in range(8)]
            for g in range(2):
                s2, r2 = B.ws_next(B.wspecs["w2"], g)
                for pi in range(4):
                    ci = g * 4 + pi
                    bank = B.next_bank()
                    B.mm_chain(bank, T, [(s2[:, (pi * 8 + k) * 128:(pi * 8 + k + 1) * 128],
                                          zb[:, k * T:(k + 1) * T]) for k in range(8)], reads=[r2] + z_res)
                    hk = h_ext[:, ci * TE + HALO:ci * TE + TE]
                    S.add("dve", lambda e, hk=hk, bank=bank, ci=ci: e.scalar_tensor_tensor(
                        hk, B.ps[bank][:, 0:T], B.cc("b_pw2", ci), hk, ALU.add, ALU.add),
                        reads=[("ps", bank), "h"], writes=["h"])
            if DBG_H1 == 1:
                store_h1(j)
            B.ffn_ple(0, h_ext, TE, HALO, pTb, after_ffn=(lambda j=j: store_h1(j)) if DBG_H1 == 2 else None)
            if DBG_H1 == 3:
                store_h1(j)
            B.rms_stats(h_ext, [(HALO, T)], B.sqb, B.rstd, "h")
            B.normalize(h_ext, B.hn, B.rstd, TE, HALO, T, "h", ("rstd", HALO))
            qkb = zb
            for g in range(4):
                sq_, rq = B.ws_next(B.wspecs["wqk"], g)
                for pi in range(4):
                    ci = g * 4 + pi
                    bank = B.next_bank()
                    B.mm_chain(bank, T, [(sq_[:, (pi * 8 + k) * 128:(pi * 8 + k + 1) * 128],
                                          B.hn[:, k * TE + HALO:k * TE + TE]) for k in range(8)],
                               reads=[rq] + hn_main)
                    sc, scr = B.next_sc()
                    S.add("act", lambda e, sc=sc, bank=bank: e.activation(sc[:, 0:T], B.ps[bank][:, 0:T], AF.Copy),
                          reads=[("ps", bank)], writes=[scr])
                    sq2 = ysq[:, (ci % 8) * T:(ci % 8 + 1) * T]
                    S.add("act", lambda e, sq2=sq2, bank=bank: e.activation(sq2, B.ps[bank][:, 0:T], AF.Square),
                          reads=[("ps", bank)], writes=[("ysq", ci % 8)])
                    b2 = B.next_bank()
                    B.mm_chain(b2, T, [(B.blk64b, sq2)], reads=["constb", ("ysq", ci % 8)])
                    sc2, scr2 = B.next_sc()
                    S.add("act", lambda e, sc2=sc2, b2=b2: e.activation(
                        sc2[:, 0:T], B.ps[b2][:, 0:T], AF.Sqrt, bias=B.cc("eps"), scale=1.0),
                        reads=[("ps", b2)], writes=[scr2])
                    S.add("dve", lambda e, sc2=sc2: e.reciprocal(sc2[:, 0:T], sc2[:, 0:T]),
                          reads=[scr2], writes=[scr2])
                    gcol = gq8 if ci < 8 else B.cc("k_g")
                    qo = qkb[:, (ci % 8) * T:(ci % 8 + 1) * T]
                    S.add("dve", lambda e, qo=qo, sc=sc, sc2=sc2, gcol=gcol: e.scalar_tensor_tensor(
                        qo, sc[:, 0:T], gcol, sc2[:, 0:T], ALU.mult, ALU.mult),
                        reads=[scr, scr2, "gq8", "cst"], writes=[("z", ci % 8)])
                if g == 1 or g == 3:
                    dst = q_out if g == 1 else k_out
                    nm = "qo" if g == 1 else "ko"
                    S.add("pool", lambda e, dst=dst, j=j: e.dma_start(
                        out=dst[j].rearrange("k p t -> p k t"), in_=qkb.rearrange("p (k t) -> p k t", t=T)),
                        reads=[("z", k) for k in range(8)], writes=[(nm, j)], dma="qko")
            for cg in range(2):
                sv, rv = B.ws_next(B.wspecs["wv"], cg)
                for tb in range(4):
                    bank = B.next_bank()
                    B.mm_chain(bank, 512, [(B.hn[:, k * TE + HALO + tb * 128:k * TE + HALO + (tb + 1) * 128],
                                            sv[:, k * 512:(k + 1) * 512]) for k in range(8)],
                               reads=[rv] + hn_main)
                    vo = vt[:, tb * NH * 129:(tb + 1) * NH * 129].rearrange("p (h c) -> p h c", c=129)[:, cg * 4:(cg + 1) * 4, 0:128]
                    S.add("dve", lambda e, vo=vo, bank=bank: e.tensor_copy(
                        vo, B.ps[bank][:, 0:512].rearrange("p (h c) -> p h c", c=128)),
                        reads=[("ps", bank)], writes=["vt"])
            S.add("pool", lambda e, j=j: e.dma_start(
                out=v_out[j].rearrange("b p c -> p b c"), in_=vt.rearrange("p (b c) -> p b c", c=NH * 129)),
                reads=["vt"], writes=[("vo", j)], dma="vo")


def prog2_io(B, NT, fused, io1=None):
    nc = B.nc
    io = {}
    sfx = "1" if fused else ""
    if fused:
        io["h1"] = io1["h1"]
        io["qT"] = io1["qT"]
        io["kp_list"] = [nc.dram_tensor(f"kTp{j}", [2 * 8 * 128, T], BF16).ap() for j in range(NT)]
        io["vp_list"] = [nc.dram_tensor(f"vp{j}", [2 * 4 * 128, NH * 129], BF16).ap() for j in range(NT)]
        kpv = [a.rearrange("(r h p) t -> r h p t", r=2, p=128) for a in io["kp_list"]]
        vpv = [a.rearrange("(r b p) c -> r b p c", r=2, p=128) for a in io["vp_list"]]
        io["kin"] = lambda r_, j_, h: kpv[j_][r_, h, :, :]
        io["vin"] = lambda r_, j_, h: vpv[j_][r_, :, :, h * 129:(h + 1) * 129]
        io["kTp"] = None
        io["vp"] = None
    else:
        io["h1"] = B.dram_in("h1", [NT, 8, 128, T])
        io["qT"] = B.dram_in("qT", [NT, 8, 128, T], BF16)
        io["kTp"] = B.dram_in("kTp", [2, NT, 8, 128, T], BF16)
        io["vp"] = B.dram_in("vp", [2, NT, 4, 128, NH * 129], BF16)
        io["kin"] = lambda r_, j_, h: io["kTp"][r_, j_, h, :, :]
        io["vin"] = lambda r_, j_, h: io["vp"][r_, j_, :, :, h * 129:(h + 1) * 129]
    io["pT"] = B.dram_in("pT1" if fused else "pT", [NT, 2, 128, T])
    io["w_o"] = B.dram_in("attn_w_o", [D, D])
    io["w_gate"] = B.dram_in("ffn_w_gate" + sfx, [D, DFF])
    io["w_up"] = B.dram_in("ffn_w_up" + sfx, [D, DFF])
    io["w_down"] = B.dram_in("ffn_w_down" + sfx, [DFF, D])
    io["w_pg"] = B.dram_in("ple_w_gate" + sfx, [D, D])
    io["w_pp"] = B.dram_in("ple_w_proj" + sfx, [256, D])
    io["out"] = B.dram_out("outT", [NT, 8, 128, T])
    io["vz_t"] = nc.dram_tensor("vz", [8, 384], F32)
    return io


def prog2_specs(io):
    return [
        WSpec("wo", io["w_o"], 0, 8, 0, D, False, "wo_scale", 512),
        WSpec("wg1", io["w_gate"], 0, 8, 0, DFF, False, "ffn_ng1", 512),
        WSpec("wu1", io["w_up"], 0, 8, 0, DFF, False, "ffn_ng1", 512),
        WSpec("wd1", io["w_down"], 0, 22, 0, D, False, None, 128),
        WSpec("pg1", io["w_pg"], 0, 8, 0, D, False, "ple_ng1", 512),
        WSpec("pp1", io["w_pp"], 0, 2, 0, D, False, None, 1024),
    ]


def p2_sizes(NT):
    NKB = 8 * NT
    SEQ = NKB * 128
    f = max(2 * 2048 + 9 * 512 + 512 + 128 + 256 + 64, 8 * T + 4 * T + T + 2 * T + 64)
    b = max(2 * SEQ + 2 * (NKB * 129 + 16) + 2 * NT * T + 2 * 9 * 512 + 6 * 512 + 512 + 2 * 512 + 64,
            NSLOT * SLOT + 8 * T + 8 * T + 22 * T + 2 * T + 8 * T + 64)
    return f, b


def shared_setup(B, es, nf32, nbf, Lc):
    nc, S = B.nc, B.S
    B.fa = Arena(es.enter_context(nc.sbuf_tensor("fa", [128, nf32], F32)), nf32)
    B.ba = Arena(es.enter_context(nc.sbuf_tensor("ba", [128, nbf], BF16)), nbf)
    fa, ba = B.fa, B.ba
    B.cst = fa.alloc(B.L.n)[:, 0:B.L.n]
    B.constb = ba.alloc(512)
    B.onesb = B.constb[:, 0:128]
    B.identb = B.constb[:, 128:256]
    B.blk64b = B.constb[:, 256:384]
    S.add("sp", lambda e: e.dma_start(out=B.cst[:, 0:Lc], in_=B.cst_in), writes=["cst"], dma="cst")
    S.add("pool", lambda e: e.memset(B.onesb, 1.0 / D), writes=["constb"])
    S.add("dve", lambda e: e.tensor_copy(B.identb, B.cc("ident", 0, 128)), reads=["cst"], writes=["constb"])
    S.add("dve", lambda e: e.tensor_copy(B.blk64b, B.cc("blk64", 0, 128)), reads=["cst"], writes=["constb"])
    S.add("dve", lambda e: e.tensor_scalar_mul(B.cc("wo_scale"), B.cc("sub_g"),
                                               math.sqrt(128.0) * (1.0 - LAMBDA_INIT)),
          reads=["cst"], writes=["cst"])
    B.ps = [es.enter_context(nc.psum_tensor(f"ps{i}", [128, 512], F32)) for i in range(7)]
    B.psT = es.enter_context(nc.psum_tensor("psT", [128, 1024], BF16))
    B.banks = list(range(7))
    B.small = fa.alloc(64)


def build_prog2(NT):
    B = Builder(NT, 2)
    nc, S = B.nc, B.S
    B.cst_in = B.dram_in("cst", [128, B.L.n])
    Lc = B.L.n
    B.L.add("wo_scale", 1)
    io = prog2_io(B, NT, False)
    specs = prog2_specs(io)
    B.wspecs = {w.name: w for w in specs}
    B.ws_init(([("wo", 0), ("wo", 1)] + B.ffn_plan(1)) * NT)
    f2, b2 = p2_sizes(NT)
    with ExitStack() as es:
        shared_setup(B, es, B.L.n + 16 + 64 + max(8192, f2), 512 + max(8192, b2), Lc)
        fa, ba = B.fa, B.ba
        mark_f, mark_b = fa.off, ba.off
        st32 = [fa.alloc(4096), fa.alloc(4096)]
        stb = [ba.alloc(4096), ba.alloc(4096)]
        B.preconvert(specs, st32, stb)
        fa.off, ba.off = mark_f, mark_b
        prog2_body(B, io, NT)
        S.barrier()
        S.emit()
    return nc


def build_fused(NT, NB):
    B = Builder(NT, 0)
    nc, S = B.nc, B.S
    B.cst_in = B.dram_in("cst", [128, B.L.n])
    Lc = B.L.n
    B.L.add("wo_scale", 1)
    io1 = prog1_io(B, NT, True)
    io2 = prog2_io(B, NT, True, io1)
    specs = prog1_specs(io1) + prog2_specs(io2)
    B.wspecs = {w.name: w for w in specs}
    f2, b2 = p2_sizes(NT)
    with ExitStack() as es:
        shared_setup(B, es, B.L.n + 16 + 64 + max(8192, f2, P1_F32), 512 + max(8192, b2, P1_BF), Lc)
        fa, ba = B.fa, B.ba
        mark_f, mark_b = fa.off, ba.off
        st32 = [fa.alloc(4096), fa.alloc(4096)]
        stb = [ba.alloc(4096), ba.alloc(4096)]
        B.preconvert(specs, st32, stb)
        fa.off, ba.off = mark_f, mark_b
        B.ws_init(prog1_plan(B) * NT)
        prog1_body(B, io1, NT)
        S.barrier()
        fa.off, ba.off = mark_f, mark_b
        groups = [[2 * b, 2 * b + 1] for b in range(NB)]
        for j in range(NT):
            S.add("pool", lambda e, j=j: e.collective_compute(
                "AllGather", ALU.bypass, replica_groups=groups,
                ins=[io1["kT_list"][j]], outs=[io2["kp_list"][j]]),
                reads=[("ko", j)], writes=[("kpair", j)], dma=("cck", j), dma_inc=1)
            S.add("pool", None, reads=[("kpair", j)])
            S.add("pool", lambda e, j=j: e.collective_compute(
                "AllGather", ALU.bypass, replica_groups=groups,
                ins=[io1["v_list"][j]], outs=[io2["vp_list"][j]]),
                reads=[("vo", j)], writes=[("vpair", j)], dma=("ccv", j), dma_inc=1)
            S.add("pool", None, reads=[("vpair", j)])
        B.ws_init(([("wo", 0), ("wo", 1)] + B.ffn_plan(1)) * NT)
        prog2_body(B, io2, NT)
        S.barrier()
        S.emit()
    return nc


def prog2_body(B, io, NT):
    nc, S = B.nc, B.S
    fa, ba = B.fa, B.ba
    NKB = 8 * NT
    SEQ = NKB * 128
    h1_in, q_in, k_in, v_in, pT_in, out = io["h1"], io["qT"], io["kTp"], io["vp"], io["pT"], io["out"]
    vz_t = io["vz_t"]
    vz = vz_t.ap()
    small, psT = B.small, B.psT
    if True:
        on_scr = nc.dram_tensor("on_scr", [NT, NH, 128, T], BF16).ap()
        mark_f2, mark_b2 = fa.off, ba.off
        E_rev = fa.alloc(8 * 256)
        E_sb = fa.alloc(8 * 256)
        Mt = fa.alloc(9 * 512)
        otmp = fa.alloc(4 * 128)
        osq = fa.alloc(128)
        ew8 = fa.alloc(256)
        kbuf = [ba.alloc(SEQ) for _ in range(2)]
        vbuf = [ba.alloc(NKB * 129) for _ in range(2)]
        qbuf = [ba.alloc(NT * T) for _ in range(2)]
        Mh = [ba.alloc(9 * 512) for _ in range(2)]
        Pb = [[ba.alloc(512) for _ in range(3)] for _ in range(2)]
        onb = ba.alloc(4 * 128)
        onb2 = [ba.alloc(512) for _ in range(2)]
        zer = ba.alloc(16)
        S.barrier()
        lam = small[:, 0:1]
        neglam = small[:, 1:2]
        s12 = small[:, 2:4]
        cb = small[:, 8:16]
        nb8 = small[:, 16:17]
        lt = otmp[:, 0:128]
        for i_, (a_, b_) in enumerate((("lq1", "lk1"), ("lq2", "lk2"))):
            S.add("dve", lambda e, a_=a_, b_=b_, i_=i_: e.tensor_tensor(
                lt[:, i_ * 64:(i_ + 1) * 64], B.cc(a_, 0, 64), B.cc(b_, 0, 64), ALU.mult),
                reads=["cst"], writes=[("lt", i_)])
            S.add("dve", lambda e, i_=i_: e.reduce_sum(s12[:, i_:i_ + 1], lt[:, i_ * 64:(i_ + 1) * 64], AX.X),
                  reads=[("lt", i_)], writes=[("s12", i_)])
        S.add("act", lambda e: e.activation(s12, s12, AF.Exp), reads=[("s12", 0), ("s12", 1)], writes=["e12"])
        S.add("dve", lambda e: e.tensor_tensor(lam, s12[:, 0:1], s12[:, 1:2], ALU.subtract), reads=["e12"], writes=["lam"])
        S.add("dve", lambda e: e.tensor_scalar(neglam, lam, LAMBDA_INIT, -1.0, ALU.add, ALU.mult),
              reads=["lam"], writes=["neglam"])
        relb = B.cc("relb", 0, 8, 0, 32)
        S.add("pe", lambda e: e.matmul(B.ps[0][0:8, 0:256], relb, B.cc("ohd", 0, 256, 0, 32), start=True, stop=True),
              reads=["cst"], writes=[("ps", 0)])
        S.add("pe", lambda e: e.matmul(B.ps[1][:, 0:8], B.cc("sel31", 0, 128, 0, 32), relb, start=True, stop=True),
              reads=["cst"], writes=[("ps", 1)])
        S.add("dve", lambda e: e.tensor_copy(cb, B.ps[1][:, 0:8]), reads=[("ps", 1)], writes=["cb"])
        S.add("dve", lambda e: e.tensor_scalar_mul(nb8[0:8, :], B.ps[0][0:8, 255:256], -1.0),
              reads=[("ps", 0)], writes=["nb8"])
        S.add("act", lambda e: e.activation(ew8[0:8, :], B.ps[0][0:8, 0:256], AF.Exp, bias=nb8[0:8, :], scale=1.0),
              reads=[("ps", 0), "nb8"], writes=["ew8"])
        S.add("pool", lambda e: e.memset(E_rev[0:8, 0:128], 0.0), writes=["erz"])
        S.add("pool", lambda e: e.dma_start(out=vz[:, 0:127], in_=E_rev[0:8, 0:127]), reads=["erz"],
              writes=["vz0"], dma="vz0")
        S.add("pool", lambda e: e.dma_start(out=vz[:, 127:383], in_=ew8[0:8, :]), reads=["ew8"],
              writes=["vz1"], dma="vz1")
        hank = bass.AP(tensor=vz_t, offset=0, ap=[[1, 128], [384, 8], [1, 256]])
        S.add("sp", lambda e: e.dma_start(out=E_rev.rearrange("p (h c) -> p h c", c=256), in_=hank),
              reads=["vz0", "vz1"], writes=["erev", "erz"], dma="erev")
        for hp in range(4):
            S.add("pe", lambda e, hp=hp: e.matmul(B.ps[hp][:, 0:512], B.cc("antiI", 0, 128),
                                                  E_rev[:, hp * 512:(hp + 1) * 512], start=True, stop=True),
                  reads=["cst", "erev", "ew8", "cb", "nb8"], writes=[("ps", hp)])
            S.add("act", lambda e, hp=hp: e.activation(E_sb[:, hp * 512:(hp + 1) * 512], B.ps[hp][:, 0:512], AF.Copy),
                  reads=[("ps", hp)], writes=["esb"])
        S.add("pool", lambda e: e.memset(otmp[:, 0:128], 1.0), reads=[("lt", 0), ("lt", 1)],
              writes=[("lt", 0), ("lt", 1)])
        for s in range(9):
            for qs in range(4):
                o = Mt[:, s * 512 + qs * 128:s * 512 + (qs + 1) * 128]
                S.add("pool", lambda e, o=o, s=s, qs=qs: e.tensor_scalar_mul(o, otmp[:, 0:128], B.cc("bgval", s * 4 + qs)),
                      reads=[("lt", 0), "cst"], writes=["Mt"])
        S.add("pool", lambda e: e.memset(zer, 0.0), writes=["zer"])

        def load_head(h):
            hb = h % 2
            for r_ in range(2):
                for j_ in range(NT):
                    S.add("sp", lambda e, r_=r_, h=h, hb=hb, j_=j_: e.dma_start(
                        out=kbuf[hb].rearrange("p (j r t) -> p j r t", r=2, t=T)[:, j_, r_, :],
                        in_=io["kin"](r_, j_, h)),
                        reads=[("kpair", j_)], writes=[("kbuf", hb)], dma=("kbuf", hb))
                    S.add("sp", lambda e, r_=r_, h=h, hb=hb, j_=j_: e.dma_start(
                        out=vbuf[hb].rearrange("p (j r b c) -> p j r b c", r=2, b=4, c=129)[:, j_, r_, :, :],
                        in_=io["vin"](r_, j_, h).rearrange("b p c -> p b c")),
                        reads=[("vpair", j_)], writes=[("vbuf", hb)], dma=("vbuf", hb))
            S.add("sp", lambda e, h=h, hb=hb: e.dma_start(
                out=qbuf[hb].rearrange("p (j t) -> p j t", t=T),
                in_=q_in[:, h, :, :].rearrange("j p t -> p j t")),
                writes=[("qbuf", hb)], dma=("qbuf", hb))
            S.add("pool", lambda e, hb=hb: e.tensor_copy(Mh[hb], Mt), reads=["Mt"], writes=[("Mh", hb)])
            for qs in range(4):
                for (s, cname, ecol) in ((qs, "cnear", 128), (qs + 4, "cnear", 128),
                                         (qs + 1, "cdiag", 0), (qs + 5, "cdiag", 0)):
                    o = Mh[hb][:, s * 512 + qs * 128:s * 512 + (qs + 1) * 128]
                    src = E_sb[:, h * 256 + ecol:h * 256 + ecol + 128]
                    mt = Mt[:, s * 512 + qs * 128:s * 512 + (qs + 1) * 128]
                    S.add("dve", lambda e, o=o, src=src, mt=mt, s=s, qs=qs, cname=cname: e.scalar_tensor_tensor(
                        o, src, B.cc(cname, s * 4 + qs), mt, ALU.mult, ALU.add),
                        reads=["esb", "Mt", "cst"], writes=[("Mh", hb)])

        def acc_ap(c, qs):
            a = c * 4 + qs
            return B.ps[4 + a // 3][:, (a % 3) * 129:(a % 3) * 129 + 129]

        pr = 0
        load_head(0)
        for h in range(NH):
            hb = h % 2
            if h + 1 < NH:
                load_head(h + 1)
            for j in range(NT):
                nkb = 8 * j + 8
                for kb in range(nkb):
                    sbuf_ = kb % 2
                    pbuf = pr % 3
                    pr += 1
                    for c in range(2):
                        bank = c * 2 + sbuf_
                        S.add("pe", lambda e, c=c, bank=bank, kb=kb, j=j, hb=hb: e.matmul(
                            B.ps[bank][:, 0:512], kbuf[hb][c * 64:(c + 1) * 64, kb * 128:(kb + 1) * 128],
                            qbuf[hb][c * 64:(c + 1) * 64, j * T:(j + 1) * T], start=True, stop=True),
                            reads=[("kbuf", hb), ("qbuf", hb)], writes=[("ps", bank)])
                        P = Pb[c][pbuf]
                        S.add("act", lambda e, P=P, bank=bank, h=h: e.activation(
                            P, B.ps[bank][:, 0:512], AF.Exp, bias=cb[:, h:h + 1], scale=1.0),
                            reads=[("ps", bank), "cb"], writes=[("P", c, pbuf)])
                        s = kb - (8 * j - 1)
                        if s >= 0:
                            S.add("dve", lambda e, P=P, s=s, hb=hb: e.tensor_tensor(
                                P, P, Mh[hb][:, s * 512:(s + 1) * 512], ALU.mult),
                                reads=[("P", c, pbuf), ("Mh", hb)], writes=[("P", c, pbuf)])

                        def pv(e, c=c, P=P, kb=kb, hb=hb, nkb=nkb):
                            bi = None
                            for qs in range(4):
                                a = c * 4 + qs
                                bi = e.matmul(acc_ap(c, qs), P[:, qs * 128:(qs + 1) * 128],
                                              vbuf[hb][:, kb * 129:(kb + 1) * 129],
                                              start=(kb == 0 and a in (0, 3, 6)), stop=(kb == nkb - 1),
                                              skip_group_check=True)
                            return bi
                        S.add("pe", pv, reads=[("P", c, pbuf), ("vbuf", hb)], writes=["acc"])
                for qs in range(4):
                    a0 = acc_ap(0, qs)
                    a1 = acc_ap(1, qs)
                    rr = small[:, 20 + qs * 4:24 + qs * 4]
                    S.add("dve", lambda e, rr=rr, a0=a0: e.reciprocal(rr[:, 0:1], a0[:, 128:129]),
                          reads=["acc"], writes=[("rr", qs)])
                    S.add("dve", lambda e, rr=rr, a1=a1: e.reciprocal(rr[:, 1:2], a1[:, 128:129]),
                          reads=["acc"], writes=[("rr", qs)])
                    S.add("dve", lambda e, rr=rr: e.tensor_tensor(rr[:, 2:3], rr[:, 1:2], neglam, ALU.mult),
                          reads=[("rr", qs), "neglam"], writes=[("rr", qs)])
                    ot = otmp[:, qs * 128:(qs + 1) * 128]
                    S.add("dve", lambda e, ot=ot, a1=a1, rr=rr: e.tensor_scalar_mul(ot, a1[:, 0:128], rr[:, 2:3]),
                          reads=["acc", ("rr", qs)], writes=[("ot", qs)])
                    S.add("dve", lambda e, ot=ot, a0=a0, rr=rr: e.scalar_tensor_tensor(
                        ot, a0[:, 0:128], rr[:, 0:1], ot, ALU.mult, ALU.add),
                        reads=["acc", ("ot", qs)], writes=[("ot", qs)])
                    S.add("act", lambda e, ot=ot: e.activation(osq, ot, AF.Square), reads=[("ot", qs)], writes=["osq"])
                    S.add("dve", lambda e, rr=rr: e.reduce_sum(rr[:, 3:4], osq, AX.X), reads=["osq"], writes=[("rr", qs)])
                    S.add("act", lambda e, rr=rr: e.activation(rr[:, 3:4], rr[:, 3:4], AF.Sqrt, bias=B.cc("eps128"), scale=1.0),
                          reads=[("rr", qs)], writes=[("rr", qs)])
                    S.add("dve", lambda e, rr=rr: e.reciprocal(rr[:, 3:4], rr[:, 3:4]), reads=[("rr", qs)], writes=[("rr", qs)])
                    ob = onb[:, qs * 128:(qs + 1) * 128]
                    S.add("pool", lambda e, ob=ob, ot=ot, rr=rr: e.tensor_scalar_mul(ob, ot, rr[:, 3:4]),
                          reads=[("ot", qs), ("rr", qs)], writes=[("onb", qs)])
                    S.add("pe", lambda e, ob=ob, qs=qs: e.transpose(psT[:, qs * 128:(qs + 1) * 128], ob, B.identb),
                          reads=[("onb", qs), "constb"], writes=["psT"])
                ob2 = (h * NT + j) % 2
                S.add("act", lambda e, ob2=ob2: e.activation(onb2[ob2], psT[:, 0:512], AF.Copy),
                      reads=["psT"], writes=[("onb2", ob2)])
                S.add("pool", lambda e, ob2=ob2, h=h, j=j: e.dma_start(out=on_scr[j, h], in_=onb2[ob2]),
                      reads=[("onb2", ob2)], writes=[("onscr", j)], dma=("onb2", ob2))
        S.barrier()
        fa.off, ba.off = mark_f2, mark_b2
        B.wslots = [ba.alloc(SLOT) for _ in range(NSLOT)]
        on_t = ba.alloc(8 * T)
        hb_ = fa.alloc(8 * T)
        B.scs = [fa.alloc(T) for _ in range(4)]
        B.rstd = fa.alloc(T)
        p32 = fa.alloc(2 * T)
        B.sqb = ba.alloc(8 * T)
        B.hn = ba.alloc(8 * T)
        B.abuf = ba.alloc(22 * T)
        pTb = ba.alloc(2 * T)
        for j in range(NT):
            S.add("sp", lambda e, j=j: e.dma_start(out=hb_.rearrange("p (k t) -> p k t", t=T),
                                                   in_=h1_in[j].rearrange("k p t -> p k t")),
                  writes=["h"], dma="h")
            B.load_pT(pT_in, j, p32, pTb)
            S.add("sp", lambda e, j=j: e.dma_start(out=on_t.rearrange("p (k t) -> p k t", t=T),
                                                   in_=on_scr[j].rearrange("k p t -> p k t")),
                  reads=[("onscr", j)], writes=["on_t"], dma="on_t")
            on_res = ["on_t"]
            for g in range(2):
                so, ro = B.ws_next(B.wspecs["wo"], g)
                for pi in range(4):
                    ci = g * 4 + pi
                    bank = B.next_bank()
                    B.mm_chain(bank, T, [(so[:, (pi * 8 + k) * 128:(pi * 8 + k + 1) * 128],
                                          on_t[:, k * T:(k + 1) * T]) for k in range(8)],
                               reads=[ro] + on_res)
                    hk = hb_[:, ci * T:(ci + 1) * T]
                    S.add("dve", lambda e, hk=hk, bank=bank: e.tensor_tensor(hk, B.ps[bank][:, 0:T], hk, ALU.add),
                          reads=[("ps", bank), "h"], writes=["h"])
            B.ffn_ple(1, hb_, T, 0, pTb)
            S.add("pool", lambda e, j=j: e.dma_start(out=out[j].rearrange("k p t -> p k t"),
                                                     in_=hb_.rearrange("p (k t) -> p k t", t=T)),
                  reads=["h"], writes=[("out", j)], dma="outst")


_PROG_CACHE = {}


def _get_prog(which, NT):
    key = (which, NT)
    if key not in _PROG_CACHE:
        _PROG_CACHE[key] = build_prog1(NT) if which == 1 else build_prog2(NT)
    return _PROG_CACHE[key]


def run_model(inp, NB, SEQ):
    NT = SEQ // T // 2
    ncores = 2 * NB
    x = np.asarray(inp["x"], np.float32)
    p = np.asarray(inp["p"], np.float32)
    wnames1 = ["conv_w_pw1", "conv_w_pw2", "attn_w_qkv"]
    in1 = []
    in2 = []
    for c in range(ncores):
        b, r = c // 2, c % 2
        xh = np.zeros((NT, 8, 128, TE), np.float32)
        pT0 = np.zeros((NT, 2, 128, T), np.float32)
        pT1 = np.zeros((NT, 2, 128, T), np.float32)
        for j in range(NT):
            t0 = (2 * j + r) * T
            lo = t0 - HALO
            seg = np.zeros((TE, D), np.float32)
            if lo >= 0:
                seg[:] = x[b, lo:t0 + T]
            else:
                seg[HALO:] = x[b, t0:t0 + T]
            xh[j] = seg.T.reshape(8, 128, TE)
            pT0[j] = p[0, b, t0:t0 + T].T.reshape(2, 128, T)
            pT1[j] = p[1, b, t0:t0 + T].T.reshape(2, 128, T)
        cst = build_consts(inp, NT, r)
        m1 = {"cst": cst, "xh": xh, "pT": pT0}
        for nm in wnames1:
            m1[nm] = np.ascontiguousarray(np.asarray(inp[nm], np.float32)[0])
        for nm in ["ffn_w_gate", "ffn_w_up", "ffn_w_down", "ple_w_gate", "ple_w_proj"]:
            m1[nm] = np.ascontiguousarray(np.asarray(inp[nm], np.float32)[0])
        in1.append(m1)
        m2 = {"cst": cst, "pT": pT1, "attn_w_o": np.ascontiguousarray(np.asarray(inp["attn_w_o"], np.float32)[0])}
        for nm in ["ffn_w_gate", "ffn_w_up", "ffn_w_down", "ple_w_gate", "ple_w_proj"]:
            m2[nm] = np.ascontiguousarray(np.asarray(inp[nm], np.float32)[1])
        in2.append(m2)
    nc1 = _get_prog(1, NT)
    res1 = run_bass_kernel_spmd(nc1, in1, core_ids=list(range(ncores))).results
    for c in range(ncores):
        b = c // 2
        in2[c]["h1"] = res1[c]["h1"]
        in2[c]["qT"] = res1[c]["qT"]
        in2[c]["kTp"] = np.stack([res1[2 * b]["kT"], res1[2 * b + 1]["kT"]])
        in2[c]["vp"] = np.stack([res1[2 * b]["v"], res1[2 * b + 1]["v"]])
    nc2 = _get_prog(2, NT)
    res2 = run_bass_kernel_spmd(nc2, in2, core_ids=list(range(ncores))).results
    out = np.zeros((NB, SEQ, D), np.float32)
    for c in range(ncores):
        b, r = c // 2, c % 2
        o = np.asarray(res2[c]["outT"], np.float32)
        for j in range(NT):
            t0 = (2 * j + r) * T
            out[b, t0:t0 + T] = o[j].reshape(D, T).T
    return out, res1, res2


def run_fused(inp, NB, SEQ):
    NT = SEQ // T // 2
    ncores = 2 * NB
    x = np.asarray(inp["x"], np.float32)
    p = np.asarray(inp["p"], np.float32)
    shared = {}
    for nm in ["conv_w_pw1", "conv_w_pw2", "attn_w_qkv", "attn_w_o"]:
        shared[nm] = np.ascontiguousarray(np.asarray(inp[nm], np.float32)[0])
    for nm in ["ffn_w_gate", "ffn_w_up", "ffn_w_down", "ple_w_gate", "ple_w_proj"]:
        for l in range(2):
            shared[nm + str(l)] = np.ascontiguousarray(np.asarray(inp[nm], np.float32)[l])
    in_maps = []
    for c in range(ncores):
        b, r = c // 2, c % 2
        xh = np.zeros((NT, 8, 128, TE), np.float32)
        pT0 = np.zeros((NT, 2, 128, T), np.float32)
        pT1 = np.zeros((NT, 2, 128, T), np.float32)
        for j in range(NT):
            t0 = (2 * j + r) * T
            lo = t0 - HALO
            seg = np.zeros((TE, D), np.float32)
            if lo >= 0:
                seg[:] = x[b, lo:t0 + T]
            else:
                seg[HALO:] = x[b, t0:t0 + T]
            xh[j] = seg.T.reshape(8, 128, TE)
            pT0[j] = p[0, b, t0:t0 + T].T.reshape(2, 128, T)
            pT1[j] = p[1, b, t0:t0 + T].T.reshape(2, 128, T)
        m = {"cst": build_consts(inp, NT, r), "xh": xh, "pT0": pT0, "pT1": pT1}
        m.update(shared)
        in_maps.append(m)
    key = ("fused", NT, NB)
    if key not in _PROG_CACHE:
        _PROG_CACHE[key] = build_fused(NT, NB)
    res = run_bass_kernel_spmd(_PROG_CACHE[key], in_maps, core_ids=list(range(ncores))).results
    out = np.zeros((NB, SEQ, D), np.float32)
    for c in range(ncores):
        b, r = c // 2, c % 2
        o = np.asarray(res[c]["outT"], np.float32)
        for j in range(NT):
            t0 = (2 * j + r) * T
            out[b, t0:t0 + T] = o[j].reshape(D, T).T
    return out


def kernel(**inputs):
    return run_fused(inputs, 4, 8192)
```

```python
import math
from contextlib import ExitStack

import numpy as np
import ml_dtypes

import concourse.bass as bass
import concourse.mybir as mybir
from concourse.bass_utils import run_bass_kernel_spmd

F32 = mybir.dt.float32
BF16 = mybir.dt.bfloat16
AF = mybir.ActivationFunctionType
ALU = mybir.AluOpType
AX = mybir.AxisListType

D = 1024
DFF = 2816
NH = 8
T = 512
HALO = 32
TE = T + HALO
CW = 31
EPS = 1e-6
LAMBDA_INIT = 0.8 - 0.6 * math.exp(-0.3 * 1)
SLOT = 4096
NSLOT = 5
DBG_H1 = 3
DBG_SCR = False

ENGS = ("pe", "act", "dve", "pool", "sp")


class Instr:
    __slots__ = ("eng", "fn", "idx", "waits", "signal", "sigval", "dma", "dmaval", "dmainc")

    def __init__(self, eng, fn, idx, dma):
        self.eng = eng
        self.fn = fn
        self.idx = idx
        self.waits = []
        self.signal = False
        self.sigval = 0
        self.dma = dma
        self.dmaval = 0
        self.dmainc = 16


class Sched:
    def __init__(self, nc):
        self.nc = nc
        self.streams = {e: [] for e in ENGS}
        self.lastw = {}
        self.readers = {}
        self.dma_counts = {}
        self.waited = {e: {} for e in ENGS}

    def add(self, eng, fn, reads=(), writes=(), dma=None, dma_inc=16):
        lst = self.streams[eng]
        ins = Instr(eng, fn, len(lst), dma)
        if dma is not None:
            c = self.dma_counts.get(dma, 0) + 1
            self.dma_counts[dma] = c
            ins.dmaval = dma_inc * c
            ins.dmainc = dma_inc
        deps = {}
        for r in reads:
            w = self.lastw.get(r)
            if w is not None:
                deps[w] = True
        for r in writes:
            w = self.lastw.get(r)
            if w is not None:
                deps[w] = True
            for rd in self.readers.get(r, ()):
                if rd not in deps:
                    deps[rd] = False
        wd = self.waited[eng]
        for d, strong in deps.items():
            if d is ins:
                continue
            if d.dma is not None:
                if dma is not None and d.dma == dma and d.eng == eng:
                    continue
                if wd.get(d.dma, 0) >= d.dmaval:
                    continue
                wd[d.dma] = d.dmaval
                ins.waits.append(d)
            else:
                if d.eng == eng and (eng == "pe" or not strong):
                    continue
                if wd.get(d.eng, -1) >= d.idx:
                    continue
                wd[d.eng] = d.idx
                d.signal = True
                ins.waits.append(d)
        for r in writes:
            self.lastw[r] = ins
            self.readers[r] = []
        for r in reads:
            self.readers.setdefault(r, []).append(ins)
        lst.append(ins)
        return ins

    def barrier(self):
        lasts = []
        for e in ENGS:
            for cand in reversed(self.streams[e]):
                if cand.dma is None and cand.fn is not None:
                    lasts.append(cand)
                    break
        dmas = {}
        for e in ENGS:
            for ins in self.streams[e]:
                if ins.dma is not None:
                    dmas[ins.dma] = ins
        for e in ENGS:
            ins = Instr(e, None, len(self.streams[e]), None)
            wd = self.waited[e]
            for d in lasts:
                if d.eng == e:
                    continue
                if wd.get(d.eng, -1) >= d.idx:
                    continue
                wd[d.eng] = d.idx
                d.signal = True
                ins.waits.append(d)
            for key, d in dmas.items():
                if wd.get(key, 0) >= d.dmaval:
                    continue
                wd[key] = d.dmaval
                ins.waits.append(d)
            self.streams[e].append(ins)

    def emit(self):
        nc = self.nc
        for e in ENGS:
            c = 0
            for ins in self.streams[e]:
                if ins.dma is None and ins.signal:
                    c += 1
                    ins.sigval = c
        with ExitStack() as es:
            esem = {e: es.enter_context(nc.semaphore("s_" + e)) for e in ENGS}
            dsem = {k: es.enter_context(nc.semaphore("d_" + str(k))) for k in self.dma_counts}
            block = es.enter_context(nc.Block())

            def run(engobj, e):
                for ins in self.streams[e]:
                    for d in ins.waits:
                        if d.dma is not None:
                            engobj.wait_ge(dsem[d.dma], d.dmaval)
                        else:
                            engobj.wait_ge(esem[d.eng], d.sigval)
                    if ins.fn is None:
                        continue
                    bi = ins.fn(engobj)
                    if ins.dma is not None:
                        bi.then_inc(dsem[ins.dma], ins.dmainc)
                    elif ins.signal:
                        bi.then_inc(esem[e], 1)

            @block.tensor
            def _(eng):
                run(eng, "pe")

            @block.scalar
            def _(eng):
                run(eng, "act")

            @block.vector
            def _(eng):
                run(eng, "dve")

            @block.gpsimd
            def _(eng):
                run(eng, "pool")

            @block.sync
            def _(eng):
                run(eng, "sp")


class Arena:
    def __init__(self, t, n):
        self.t = t
        self.n = n
        self.off = 0

    def alloc(self, n):
        nr = (n + 15) // 16 * 16
        assert self.off + nr <= self.n, ("arena overflow", self.off, nr, self.n)
        ap = self.t[:, self.off:self.off + n]
        self.off += nr
        return ap


def _cols(v):
    v = np.asarray(v, np.float32).reshape(-1, 128)
    return np.ascontiguousarray(v.T)


def t5_bucket_np(dist):
    n = np.maximum(dist, 0)
    is_small = n < 16
    nf = np.maximum(n, 1).astype(np.float32)
    large = 16 + (np.log(nf / np.float32(16)) / np.float32(math.log(128 / 16)) * np.float32(16)).astype(np.int32)
    large = np.minimum(large, 31)
    return np.where(is_small, n, large)


class CLayout:
    def __init__(self):
        self.off = {}
        self.n = 0

    def add(self, name, w):
        self.off[name] = (self.n, w)
        self.n += w


def const_layout(NT):
    L = CLayout()
    for name, w in [
        ("b_pw1", 16), ("dw_w", 8 * CW), ("dw_b", 8), ("ln_g", 8), ("ln_b", 8), ("b_pw2", 8),
        ("conv_ng", 8), ("ffn_ng0", 8), ("ple_ng0", 8), ("attn_ng", 8), ("ffn_ng1", 8), ("ple_ng1", 8),
        ("q_g", 1), ("k_g", 1), ("sub_g", 1), ("eps", 1), ("eps128", 1),
        ("halo_valid", NT), ("lq1", 64), ("lk1", 64), ("lq2", 64), ("lk2", 64),
        ("relb", 8), ("ohd", 256), ("sel31", 128), ("ident", 128), ("antiI", 128), ("blk64", 128),
        ("bgval", 36), ("cnear", 36), ("cdiag", 36),
    ]:
        L.add(name, w)
    return L


def build_consts(inp, NT, r):
    L = const_layout(NT)
    C = np.zeros((128, L.n), np.float32)

    def put(name, arr):
        o, w = L.off[name]
        arr = np.asarray(arr, np.float32)
        C[:arr.shape[0], o:o + w] = arr.reshape(arr.shape[0], w)

    put("b_pw1", _cols(inp["conv_b_pw1"][0]))
    dw = np.asarray(inp["conv_dw_w"][0], np.float32)
    dwl = np.zeros((128, 8, CW), np.float32)
    for i in range(8):
        dwl[:, i, :] = dw[:, i * 128:(i + 1) * 128].T
    put("dw_w", dwl.reshape(128, 8 * CW))
    put("dw_b", _cols(inp["conv_dw_b"][0]))
    put("ln_g", _cols(inp["conv_ln_g"][0]))
    put("ln_b", _cols(inp["conv_ln_b"][0]))
    put("b_pw2", _cols(inp["conv_b_pw2"][0]))
    put("conv_ng", _cols(inp["conv_norm_g"][0]))
    put("ffn_ng0", _cols(inp["ffn_norm_g"][0]))
    put("ple_ng0", _cols(inp["ple_norm_g"][0]))
    put("attn_ng", _cols(inp["attn_norm_g"][0]))
    put("ffn_ng1", _cols(inp["ffn_norm_g"][1]))
    put("ple_ng1", _cols(inp["ple_norm_g"][1]))
    put("q_g", np.tile(np.asarray(inp["attn_q_norm_g"][0], np.float32), 2).reshape(128, 1))
    put("k_g", np.tile(np.asarray(inp["attn_k_norm_g"][0], np.float32), 2).reshape(128, 1))
    put("sub_g", np.asarray(inp["attn_sub_norm_g"][0], np.float32).reshape(128, 1))
    put("eps", np.full((128, 1), EPS, np.float32))
    put("eps128", np.full((128, 1), 128 * EPS, np.float32))
    hv = np.ones((128, NT), np.float32)
    if r == 0:
        hv[:, 0] = 0.0
    put("halo_valid", hv)
    for nm, key in [("lq1", "attn_lambda_q1"), ("lk1", "attn_lambda_k1"),
                    ("lq2", "attn_lambda_q2"), ("lk2", "attn_lambda_k2")]:
        put(nm, np.tile(np.asarray(inp[key][0], np.float32)[None, :], (128, 1)))
    put("relb", np.asarray(inp["rel_bias"], np.float32))
    bk = t5_bucket_np(np.arange(256, dtype=np.int32))
    oh = np.zeros((32, 256), np.float32)
    oh[bk, np.arange(256)] = 1.0
    put("ohd", oh)
    s31 = np.zeros((32, 128), np.float32)
    s31[31, :] = 1.0
    put("sel31", s31)
    put("ident", np.eye(128, dtype=np.float32))
    put("antiI", np.eye(128, dtype=np.float32)[::-1])
    b64 = np.zeros((128, 128), np.float32)
    b64[:64, :64] = 1.0 / 64
    b64[64:, 64:] = 1.0 / 64
    put("blk64", b64)
    bg = np.zeros((9, 4), np.float32)
    cn = np.zeros((9, 4), np.float32)
    cd = np.zeros((9, 4), np.float32)
    for s in range(9):
        for qs in range(4):
            delta = 4 * r + qs + 1 - s
            if delta >= 2:
                bg[s, qs] = 1.0
            elif delta == 1:
                cn[s, qs] = 1.0
            elif delta == 0:
                cd[s, qs] = 1.0
    put("bgval", np.tile(bg.reshape(1, 36), (128, 1)))
    put("cnear", np.tile(cn.reshape(1, 36), (128, 1)))
    put("cdiag", np.tile(cd.reshape(1, 36), (128, 1)))
    return C


class WSpec:
    def __init__(self, name, src, k0, kc, c0, ncols, wide, scale, gcols):
        self.name = name
        self.src = src
        self.k0 = k0
        self.kc = kc
        self.c0 = c0
        self.ncols = ncols
        self.wide = wide
        self.scale = scale
        self.gcols = gcols
        self.ngroups = (ncols + gcols - 1) // gcols
        self.scr = None

    def gsize(self, g):
        return min(self.gcols, self.ncols - g * self.gcols)


class Builder:
    def __init__(self, NT, prog):
        self.NT = NT
        self.prog = prog
        self.nc = bass.Bass("TRN2", target_bir_lowering=False)
        self.S = Sched(self.nc)
        self.L = const_layout(NT)
        self.bank_rr = 0
        self.sc_rr = 0

    def cc(self, name, j=0, w=1, p0=0, p1=128):
        o, _ = self.L.off[name]
        return self.cst[p0:p1, o + j:o + j + w]

    def next_bank(self):
        b = self.banks[self.bank_rr % len(self.banks)]
        self.bank_rr += 1
        return b

    def next_sc(self):
        i = self.sc_rr % len(self.scs)
        self.sc_rr += 1
        return self.scs[i], ("sc", i)

    def dram_in(self, name, shape, dt=F32):
        return self.nc.dram_tensor(name, list(shape), dt, kind="ExternalInput").ap()

    def dram_out(self, name, shape, dt=F32):
        return self.nc.dram_tensor(name, list(shape), dt, kind="ExternalOutput").ap()

    def preconvert(self, specs, st32, stb):
        S = self.S
        n = 0
        for ws in specs:
            ws.scr = self.nc.dram_tensor("scr_" + ws.name, [ws.ngroups, 128, SLOT], BF16,
                                         kind="ExternalOutput" if DBG_SCR else "Internal").ap()
            srcv = ws.src[ws.k0:ws.k0 + ws.kc * 128, :].rearrange("(k p) n -> p k n", p=128)
            for g in range(ws.ngroups):
                gc = ws.gsize(g)
                c0 = ws.c0 + g * ws.gcols
                b = n % 2
                n += 1
                s32 = st32[b][:, 0:ws.kc * gc]
                sb = stb[b][:, 0:ws.kc * gc]
                S.add("sp", lambda e, s32=s32, srcv=srcv, c0=c0, gc=gc: e.dma_start(
                    out=s32.rearrange("p (k n) -> p k n", n=gc), in_=srcv[:, :, c0:c0 + gc]),
                    writes=[("st32", b)], dma=("st32", b))
                for k in range(ws.kc):
                    eng = "dve" if (k % 2 == 0) else "pool"
                    if ws.wide:
                        o_ap = sb[:, k * gc:(k + 1) * gc]
                        i_ap = s32[:, k * gc:(k + 1) * gc]
                    else:
                        npan = gc // 128
                        o_ap = sb.rearrange("p (n k c) -> p n k c", k=ws.kc, c=128)[:, :, k, :]
                        i_ap = s32.rearrange("p (k n c) -> p k n c", n=npan, c=128)[:, k, :, :]
                    if ws.scale is None:
                        S.add(eng, lambda e, o_ap=o_ap, i_ap=i_ap: e.tensor_copy(o_ap, i_ap),
                              reads=[("st32", b)], writes=[("stb", b, k)])
                    else:
                        sc = self.cc(ws.scale, k if ws.scale != "wo_scale" else 0)
                        S.add(eng, lambda e, o_ap=o_ap, i_ap=i_ap, sc=sc: e.tensor_scalar_mul(o_ap, i_ap, sc),
                              reads=[("st32", b), "cst"], writes=[("stb", b, k)])
                S.add("act", lambda e, ws=ws, g=g, sb=sb, gc=gc: e.dma_start(
                    out=ws.scr[g, :, 0:ws.kc * gc], in_=sb),
                    reads=[("stb", b, k) for k in range(ws.kc)], writes=[("scr", ws.name, g)],
                    dma=("stb", b))

    def ws_init(self, plan):
        self.plan = plan
        self.ws_cur = 0
        self.ws_emitted = 0

    def ws_next(self, ws, g):
        S = self.S
        assert self.plan[self.ws_cur] == (ws.name, g), (self.plan[self.ws_cur], ws.name, g)
        hi = min(len(self.plan), self.ws_cur + NSLOT - 1)
        while self.ws_emitted < hi:
            m = self.ws_emitted
            nm, gg = self.plan[m]
            w2 = self.wspecs[nm]
            sl = m % NSLOT
            ne = w2.kc * w2.gsize(gg)
            S.add("sp", lambda e, sl=sl, w2=w2, gg=gg, ne=ne: e.dma_start(
                out=self.wslots[sl][:, 0:ne], in_=w2.scr[gg, :, 0:ne]),
                reads=[("scr", nm, gg)], writes=[("wslot", sl)], dma=("wslot", sl))
            self.ws_emitted += 1
        sl = self.ws_cur % NSLOT
        self.ws_cur += 1
        return self.wslots[sl], ("wslot", sl)

    def mm_chain(self, bank, ncols, pairs, reads, col0=0):
        out = self.ps[bank][:, col0:col0 + ncols]
        n = len(pairs)

        def fn(e):
            bi = None
            for i, (l, r) in enumerate(pairs):
                bi = e.matmul(out, l, r, start=(i == 0), stop=(i == n - 1))
            return bi
        self.S.add("pe", fn, reads=reads, writes=[("ps", bank)])

    def rms_stats(self, src, ncols_list, sqb, rstd, src_res):
        S = self.S
        W = sqb.shape[1] // 8
        S.add("act", lambda e: e.activation(sqb, src, AF.Square), reads=[src_res], writes=["sqb"])
        for (c0, n) in ncols_list:
            bank = self.next_bank()
            self.mm_chain(bank, n, [(self.onesb, sqb[:, k * W + c0:k * W + c0 + n]) for k in range(8)],
                          reads=["sqb", "constb"])
            o = rstd[:, c0:c0 + n]
            S.add("act", lambda e, o=o, bank=bank, n=n: e.activation(
                o, self.ps[bank][:, 0:n], AF.Sqrt, bias=self.cc("eps"), scale=1.0),
                reads=[("ps", bank)], writes=[("rstd", c0)])
            S.add("dve", lambda e, o=o: e.reciprocal(o, o), reads=[("rstd", c0)], writes=[("rstd", c0)])

    def normalize(self, src, hn, rstd, W, c0, n, src_res, rres):
        S = self.S
        for k in range(8):
            eng = "dve" if k % 2 == 0 else "pool"
            o = hn[:, k * W + c0:k * W + c0 + n]
            i = src[:, k * W + c0:k * W + c0 + n]
            rr = rstd[:, c0:c0 + n]
            S.add(eng, lambda e, o=o, i=i, rr=rr: e.tensor_tensor(o, i, rr, ALU.mult),
                  reads=[src_res, rres], writes=[("hn", k, c0)])

    def ffn_ple(self, layer, hfull, HW, hoff, pTb, after_ffn=None):
        S = self.S
        W = self.wspecs
        sfx = str(layer)
        hres = "h"

        def hmain(k):
            return hfull[:, k * HW + hoff:k * HW + hoff + T]
        sqb = self.sqb[:, 0:8 * HW]
        self.rms_stats(hfull, [(hoff, T)], sqb, self.rstd, hres)
        self.normalize(hfull, self.hn, self.rstd, HW, hoff, T, hres, ("rstd", hoff))
        hn_res = [("hn", k, hoff) for k in range(8)]

        def hnk(k):
            return self.hn[:, k * HW + hoff:k * HW + hoff + T]
        wg, wu = W["wg" + sfx], W["wu" + sfx]
        for g in range(wg.ngroups):
            sg, rg = self.ws_next(wg, g)
            su, ru = self.ws_next(wu, g)
            npan = wg.gsize(g) // 128
            for pi in range(npan):
                ci = g * 4 + pi
                bg_ = self.next_bank()
                bu_ = self.next_bank()
                self.mm_chain(bg_, T, [(sg[:, (pi * 8 + k) * 128:(pi * 8 + k + 1) * 128], hnk(k)) for k in range(8)],
                              reads=[rg] + hn_res)
                self.mm_chain(bu_, T, [(su[:, (pi * 8 + k) * 128:(pi * 8 + k + 1) * 128], hnk(k)) for k in range(8)],
                              reads=[ru] + hn_res)
                sc, scr = self.next_sc()
                S.add("act", lambda e, sc=sc, bg_=bg_: e.activation(sc[:, 0:T], self.ps[bg_][:, 0:T], AF.Silu),
                      reads=[("ps", bg_)], writes=[scr])
                a = self.abuf[:, ci * T:(ci + 1) * T]
                S.add("dve", lambda e, a=a, bu_=bu_, sc=sc: e.tensor_tensor(a, self.ps[bu_][:, 0:T], sc[:, 0:T], ALU.mult),
                      reads=[("ps", bu_), scr], writes=[("a", ci)])
        wd = W["wd" + sfx]
        a_res = [("a", ci) for ci in range(22)]
        for g in range(8):
            sd, rd = self.ws_next(wd, g)
            bank = self.next_bank()
            self.mm_chain(bank, T, [(sd[:, k * 128:(k + 1) * 128], self.abuf[:, k * T:(k + 1) * T]) for k in range(22)],
                          reads=[rd] + a_res)
            hk = hmain(g)
            S.add("dve", lambda e, hk=hk, bank=bank: e.tensor_tensor(hk, self.ps[bank][:, 0:T], hk, ALU.add),
                  reads=[("ps", bank), hres], writes=[hres])
        if after_ffn is not None:
            after_ffn()
        self.rms_stats(hfull, [(hoff, T)], sqb, self.rstd, hres)
        self.normalize(hfull, self.hn, self.rstd, HW, hoff, T, hres, ("rstd", hoff))
        pg, pp = W["pg" + sfx], W["pp" + sfx]
        spp, rpp = None, None
        for g in range(2):
            sgt, rgt = self.ws_next(pg, g)
            if g == 0:
                spp, rpp = self.ws_next(pp, 0)
            for pi in range(4):
                ci = g * 4 + pi
                bg_ = self.next_bank()
                bp_ = self.next_bank()
                self.mm_chain(bg_, T, [(sgt[:, (pi * 8 + k) * 128:(pi * 8 + k + 1) * 128], hnk(k)) for k in range(8)],
                              reads=[rgt] + hn_res)
                self.mm_chain(bp_, T, [(spp[:, (ci * 2 + k) * 128:(ci * 2 + k + 1) * 128], pTb[:, k * T:(k + 1) * T]) for k in range(2)],
                              reads=[rpp, "pTb"])
                sc, scr = self.next_sc()
                S.add("act", lambda e, sc=sc, bg_=bg_: e.activation(sc[:, 0:T], self.ps[bg_][:, 0:T], AF.Sigmoid),
                      reads=[("ps", bg_)], writes=[scr])
                S.add("dve", lambda e, sc=sc, bp_=bp_: e.tensor_tensor(sc[:, 0:T], self.ps[bp_][:, 0:T], sc[:, 0:T], ALU.mult),
                      reads=[("ps", bp_), scr], writes=[scr])
                hk = hmain(ci)
                S.add("pool", lambda e, hk=hk, sc=sc: e.tensor_tensor(hk, hk, sc[:, 0:T], ALU.add),
                      reads=[scr, hres], writes=[hres])

    def ffn_plan(self, layer):
        sfx = str(layer)
        pl = []
        for g in range(6):
            pl += [("wg" + sfx, g), ("wu" + sfx, g)]
        pl += [("wd" + sfx, g) for g in range(8)]
        pl += [("pg" + sfx, 0), ("pp" + sfx, 0), ("pg" + sfx, 1)]
        return pl

    def common_setup(self, es, nf32, nbf):
        nc = self.nc
        self.fa = Arena(es.enter_context(nc.sbuf_tensor("fa", [128, nf32], F32)), nf32)
        self.ba = Arena(es.enter_context(nc.sbuf_tensor("ba", [128, nbf], BF16)), nbf)
        self.cst = self.fa.alloc(self.L.n)[:, 0:self.L.n]
        self.constb = self.ba.alloc(128 * 4)
        self.onesb = self.constb[:, 0:128]
        self.identb = self.constb[:, 128:256]
        self.blk64b = self.constb[:, 256:384]
        S = self.S
        S.add("sp", lambda e: e.dma_start(out=self.cst, in_=self.cst_in), writes=["cst"], dma="cst")
        S.add("pool", lambda e: e.memset(self.onesb, 1.0 / D), writes=["constb"])
        S.add("dve", lambda e: e.tensor_copy(self.identb, self.cc("ident", 0, 128)), reads=["cst"], writes=["constb"])
        S.add("dve", lambda e: e.tensor_copy(self.blk64b, self.cc("blk64", 0, 128)), reads=["cst"], writes=["constb"])

    def load_pT(self, pT_in, j, p32, pTb):
        S = self.S
        S.add("sp", lambda e: e.dma_start(out=p32.rearrange("p (k t) -> p k t", t=T),
                                          in_=pT_in[j].rearrange("k p t -> p k t")),
              writes=["p32"], dma="p32")
        S.add("pool", lambda e: e.tensor_copy(pTb, p32), reads=["p32"], writes=["pTb"])


def prog1_io(B, NT, fused):
    io = {}
    io["xh"] = B.dram_in("xh", [NT, 8, 128, TE])
    io["pT"] = B.dram_in("pT0" if fused else "pT", [NT, 2, 128, T])
    sfx = "0" if fused else ""
    io["w_pw1"] = B.dram_in("conv_w_pw1", [D, 2 * D])
    io["w_pw2"] = B.dram_in("conv_w_pw2", [D, D])
    io["w_gate"] = B.dram_in("ffn_w_gate" + sfx, [D, DFF])
    io["w_up"] = B.dram_in("ffn_w_up" + sfx, [D, DFF])
    io["w_down"] = B.dram_in("ffn_w_down" + sfx, [DFF, D])
    io["w_pg"] = B.dram_in("ple_w_gate" + sfx, [D, D])
    io["w_pp"] = B.dram_in("ple_w_proj" + sfx, [256, D])
    io["w_qkv"] = B.dram_in("attn_w_qkv", [D, 3 * D])
    if fused:
        nc = B.nc
        io["h1"] = nc.dram_tensor("h1", [NT, 8, 128, T], F32).ap()
        io["qT"] = nc.dram_tensor("qT", [NT, 8, 128, T], BF16).ap()
        io["kT_list"] = [nc.dram_tensor(f"kT{j}", [8 * 128, T], BF16).ap() for j in range(NT)]
        io["v_list"] = [nc.dram_tensor(f"v{j}", [4 * 128, NH * 129], BF16).ap() for j in range(NT)]
        io["kT"] = [a.rearrange("(h p) t -> h p t", p=128) for a in io["kT_list"]]
        io["v"] = [a.rearrange("(b p) c -> b p c", p=128) for a in io["v_list"]]
    else:
        io["h1"] = B.dram_out("h1", [NT, 8, 128, T])
        io["qT"] = B.dram_out("qT", [NT, 8, 128, T], BF16)
        io["kT"] = B.dram_out("kT", [NT, 8, 128, T], BF16)
        io["v"] = B.dram_out("v", [NT, 4, 128, NH * 129], BF16)
    return io


def prog1_specs(io):
    return [
        WSpec("w1a", io["w_pw1"], 0, 8, 0, D, False, "conv_ng", 512),
        WSpec("w1b", io["w_pw1"], 0, 8, D, D, False, "conv_ng", 512),
        WSpec("w2", io["w_pw2"], 0, 8, 0, D, False, None, 512),
        WSpec("wg0", io["w_gate"], 0, 8, 0, DFF, False, "ffn_ng0", 512),
        WSpec("wu0", io["w_up"], 0, 8, 0, DFF, False, "ffn_ng0", 512),
        WSpec("wd0", io["w_down"], 0, 22, 0, D, False, None, 128),
        WSpec("pg0", io["w_pg"], 0, 8, 0, D, False, "ple_ng0", 512),
        WSpec("pp0", io["w_pp"], 0, 2, 0, D, False, None, 1024),
        WSpec("wqk", io["w_qkv"], 0, 8, 0, 2 * D, False, "attn_ng", 512),
        WSpec("wv", io["w_qkv"], 0, 8, 2 * D, D, True, "attn_ng", 512),
    ]


def prog1_plan(B):
    plan_tile = [("w1a", 0), ("w1b", 0), ("w1a", 1), ("w1b", 1), ("w2", 0), ("w2", 1)]
    plan_tile += B.ffn_plan(0)
    plan_tile += [("wqk", g) for g in range(4)] + [("wv", 0), ("wv", 1)]
    return plan_tile


P1_F32 = 64 + 8 * TE + 8 * T + 9 * TE + 2 * T + 64
P1_BF = NSLOT * SLOT + 2 * CW * 128 + 3 * 8 * TE + 3 * 8 * T + 2 * T + 4 * 1040 + 64


def build_prog1(NT):
    B = Builder(NT, 1)
    nc, S = B.nc, B.S
    B.cst_in = B.dram_in("cst", [128, B.L.n])
    io = prog1_io(B, NT, False)
    specs = prog1_specs(io)
    B.wspecs = {w.name: w for w in specs}
    B.ws_init(prog1_plan(B) * NT)
    with ExitStack() as es:
        B.common_setup(es, nf32=B.L.n + 16 + P1_F32, nbf=512 + P1_BF)
        B.ps = [es.enter_context(nc.psum_tensor(f"ps{i}", [128, 512], F32)) for i in range(8)]
        B.banks = list(range(8))
        fa, ba = B.fa, B.ba
        mark_f, mark_b = fa.off, ba.off
        st32 = [fa.alloc(4096), fa.alloc(4096)]
        stb = [ba.alloc(4096), ba.alloc(4096)]
        B.preconvert(specs, st32, stb)
        fa.off, ba.off = mark_f, mark_b
        prog1_body(B, io, NT)
        S.barrier()
        S.emit()
    return nc


def prog1_body(B, io, NT):
    nc, S = B.nc, B.S
    fa, ba = B.fa, B.ba
    xh, pT_in = io["xh"], io["pT"]
    h1_out, q_out, k_out, v_out = io["h1"], io["qT"], io["kT"], io["v"]
    if True:
        B.wslots = [ba.alloc(SLOT) for _ in range(NSLOT)]
        dbuf = [ba.alloc(CW * 128) for _ in range(2)]
        h_ext = fa.alloc(8 * TE)
        y32 = fa.alloc(8 * T)
        B.scs = [fa.alloc(TE) for _ in range(4)]
        B.rstd = fa.alloc(TE)
        mean_sb = fa.alloc(TE)
        rstd2 = fa.alloc(TE)
        nmr = fa.alloc(TE)
        tmpv = fa.alloc(TE)
        p32 = fa.alloc(2 * T)
        B.sqb = ba.alloc(8 * TE)
        B.hn = ba.alloc(8 * TE)
        u_ext = ba.alloc(8 * TE)
        B.abuf = ba.alloc(24 * T)
        ybf = B.abuf[:, 0:8 * T]
        ysq = B.abuf[:, 8 * T:16 * T]
        zb = B.abuf[:, 16 * T:24 * T]
        pTb = ba.alloc(2 * T)
        vt = ba.alloc(4 * NH * 129)
        hvt = fa.alloc(16)
        S.barrier()
        S.add("pool", lambda e: e.memset(vt, 1.0), writes=["vt"])
        gq8 = hvt[:, 0:1]
        S.add("dve", lambda e: e.tensor_scalar_mul(gq8, B.cc("q_g"), 0.125), reads=["cst"], writes=["gq8"])

        def store_h1(j):
            S.add("pool", lambda e, j=j: e.dma_start(
                out=h1_out[j].rearrange("k p t -> p k t"),
                in_=h_ext.rearrange("p (k t) -> p k t", t=TE)[:, :, HALO:TE]),
                reads=["h"], writes=[("h1o", j)], dma="h1o")

        def dbg_dump(j, srcfn, reads):
            for k in range(8):
                S.add("dve", lambda e, k=k: e.tensor_copy(h_ext[:, k * TE + HALO:k * TE + TE], srcfn(k)),
                      reads=list(reads) + ["h"], writes=["h"])
            store_h1(j)

        for j in range(NT):
            S.add("sp", lambda e, j=j: e.dma_start(out=h_ext.rearrange("p (k t) -> p k t", t=TE),
                                                   in_=xh[j].rearrange("k p t -> p k t")),
                  writes=["h"], dma="h")
            B.load_pT(pT_in, j, p32, pTb)
            if DBG_H1 == 0:
                store_h1(j)
            B.rms_stats(h_ext, [(HALO, T), (0, HALO)], B.sqb, B.rstd, "h")
            B.normalize(h_ext, B.hn, B.rstd, TE, HALO, T, "h", ("rstd", HALO))
            B.normalize(h_ext, B.hn, B.rstd, TE, 0, HALO, "h", ("rstd", 0))
            hn_main = [("hn", k, HALO) for k in range(8)]
            hn_halo = [("hn", k, 0) for k in range(8)]
            if DBG_H1 == 10:
                dbg_dump(j, lambda k: B.hn[:, k * TE + HALO:k * TE + TE], hn_main)
            for g in range(2):
                sa, ra = B.ws_next(B.wspecs["w1a"], g)
                sb_, rb = B.ws_next(B.wspecs["w1b"], g)
                for pi in range(4):
                    ci = g * 4 + pi
                    for (c0, n, hres_) in ((HALO, T, hn_main), (0, HALO, hn_halo)):
                        ba_ = B.next_bank()
                        bb_ = B.next_bank()
                        B.mm_chain(ba_, n, [(sa[:, (pi * 8 + k) * 128:(pi * 8 + k + 1) * 128],
                                             B.hn[:, k * TE + c0:k * TE + c0 + n]) for k in range(8)],
                                   reads=[ra] + hres_)
                        B.mm_chain(bb_, n, [(sb_[:, (pi * 8 + k) * 128:(pi * 8 + k + 1) * 128],
                                             B.hn[:, k * TE + c0:k * TE + c0 + n]) for k in range(8)],
                                   reads=[rb] + hres_)
                        sc, scr = B.next_sc()
                        S.add("act", lambda e, sc=sc, bb_=bb_, n=n, ci=ci: e.activation(
                            sc[:, 0:n], B.ps[bb_][:, 0:n], AF.Sigmoid, bias=B.cc("b_pw1", 8 + ci), scale=1.0),
                            reads=[("ps", bb_)], writes=[scr])
                        uo = u_ext[:, ci * TE + c0:ci * TE + c0 + n]
                        if DBG_H1 == 16:
                            S.add("act", lambda e, uo=uo, ba_=ba_, n=n: e.activation(uo, B.ps[ba_][:, 0:n], AF.Copy),
                                  reads=[("ps", ba_), scr], writes=[("u", ci, c0)])
                            continue
                        if DBG_H1 == 17:
                            S.add("act", lambda e, uo=uo, sa=sa, n=n, pi=pi: e.activation(uo, sa[:, pi * 1024:pi * 1024 + n], AF.Copy),
                                  reads=[ra, ("ps", ba_), scr], writes=[("u", ci, c0)])
                            continue
                        S.add("dve", lambda e, uo=uo, ba_=ba_, sc=sc, n=n, ci=ci: e.scalar_tensor_tensor(
                            uo, B.ps[ba_][:, 0:n], B.cc("b_pw1", ci), sc[:, 0:n], ALU.add, ALU.mult),
                            reads=[("ps", ba_), scr], writes=[("u", ci, c0)])
                        if c0 == 0:
                            S.add("pool", lambda e, uo=uo, j=j: e.tensor_scalar_mul(uo, uo, B.cc("halo_valid", j)),
                                  reads=[("u", ci, 0)], writes=[("u", ci, 0)])
            if DBG_H1 in (11, 16, 17):
                dbg_dump(j, lambda k: u_ext[:, k * TE + HALO:k * TE + TE], [("u", k, HALO) for k in range(8)])
            for ci in range(8):
                db = ci % 2
                for k in range(CW):
                    o = dbuf[db][:, k * 128:(k + 1) * 128]
                    sc = B.cc("dw_w", ci * CW + k)
                    eng = "dve" if (k % 2 == 0) else "pool"
                    S.add(eng, lambda e, o=o, sc=sc: e.tensor_scalar_mul(o, B.cc("ident", 0, 128), sc),
                          reads=["cst"], writes=[("dmat", db, k)])
                bank = B.next_bank()
                B.mm_chain(bank, T, [(dbuf[db][:, k * 128:(k + 1) * 128],
                                      u_ext[:, ci * TE + k + 2:ci * TE + k + 2 + T]) for k in range(CW)],
                           reads=[("dmat", db, k) for k in range(CW)] + [("u", ci, 0), ("u", ci, HALO)])
                yo = y32[:, ci * T:(ci + 1) * T]
                S.add("act", lambda e, yo=yo, bank=bank, ci=ci: e.activation(
                    yo, B.ps[bank][:, 0:T], AF.Identity, bias=B.cc("dw_b", ci), scale=1.0),
                    reads=[("ps", bank)], writes=[("y", ci)])
                S.add("act", lambda e, bank=bank, ci=ci: e.activation(
                    ysq[:, ci * T:(ci + 1) * T], B.ps[bank][:, 0:T], AF.Square, bias=B.cc("dw_b", ci), scale=1.0),
                    reads=[("ps", bank)], writes=[("ysq", ci)])
                S.add("pool", lambda e, yo=yo, ci=ci: e.tensor_copy(ybf[:, ci * T:(ci + 1) * T], yo),
                      reads=[("y", ci)], writes=[("ybf", ci)])
            if DBG_H1 == 12:
                dbg_dump(j, lambda k: y32[:, k * T:(k + 1) * T], [("y", k) for k in range(8)])
            bm = B.next_bank()
            bq = B.next_bank()
            B.mm_chain(bm, T, [(B.onesb, ybf[:, k * T:(k + 1) * T]) for k in range(8)],
                       reads=["constb"] + [("ybf", k) for k in range(8)])
            B.mm_chain(bq, T, [(B.onesb, ysq[:, k * T:(k + 1) * T]) for k in range(8)],
                       reads=["constb"] + [("ysq", k) for k in range(8)])
            S.add("act", lambda e, bm=bm: e.activation(mean_sb[:, 0:T], B.ps[bm][:, 0:T], AF.Copy),
                  reads=[("ps", bm)], writes=["mean"])
            S.add("dve", lambda e: e.tensor_tensor(tmpv[:, 0:T], mean_sb[:, 0:T], mean_sb[:, 0:T], ALU.mult),
                  reads=["mean"], writes=["tmpv"])
            S.add("dve", lambda e, bq=bq: e.tensor_tensor(tmpv[:, 0:T], B.ps[bq][:, 0:T], tmpv[:, 0:T], ALU.subtract),
                  reads=[("ps", bq), "tmpv"], writes=["tmpv"])
            S.add("act", lambda e: e.activation(rstd2[:, 0:T], tmpv[:, 0:T], AF.Sqrt, bias=B.cc("eps"), scale=1.0),
                  reads=["tmpv"], writes=["rstd2"])
            S.add("dve", lambda e: e.reciprocal(rstd2[:, 0:T], rstd2[:, 0:T]), reads=["rstd2"], writes=["rstd2"])
            S.add("dve", lambda e: e.scalar_tensor_tensor(nmr[:, 0:T], mean_sb[:, 0:T], -1.0, rstd2[:, 0:T],
                                                          ALU.mult, ALU.mult),
                  reads=["mean", "rstd2"], writes=["nmr"])
            for ci in range(8):
                yo = y32[:, ci * T:(ci + 1) * T]
                S.add("dve", lambda e, yo=yo: e.tensor_tensor(yo, yo, rstd2[:, 0:T], ALU.mult),
                      reads=[("y", ci), "rstd2"], writes=[("y", ci)])
                S.add("pool", lambda e, yo=yo: e.tensor_tensor(yo, yo, nmr[:, 0:T], ALU.add),
                      reads=[("y", ci), "nmr"], writes=[("y", ci)])
                S.add("act", lambda e, yo=yo, ci=ci: e.activation(
                    zb[:, ci * T:(ci + 1) * T], yo, AF.Silu, bias=B.cc("ln_b", ci), scale=B.cc("ln_g", ci)),
                    reads=[("y", ci)], writes=[("z", ci)])
            if DBG_H1 == 13:
                dbg_dump(j, lambda k: zb[:, k * T:(k + 1) * T], [("z", k) for k in range(8)])
            z_res = [("z", k) for k in range(8)]
            for g in range(2):
                s2, r2 = B.ws_next(B.wspecs["w2"], g)
                for pi in range(4):
                    ci = g * 4 + pi
                    bank = B.next_bank()
                    B.mm_chain(bank, T, [(s2[:, (pi * 8 + k) * 128:(pi * 8 + k + 1) * 128],
                                          zb[:, k * T:(k + 1) * T]) for k in range(8)], reads=[r2] + z_res)
                    hk = h_ext[:, ci * TE + HALO:ci * TE + TE]
                    S.add("dve", lambda e, hk=hk, bank=bank, ci=ci: e.scalar_tensor_tensor(
                        hk, B.ps[bank][:, 0:T], B.cc("b_pw2", ci), hk, ALU.add, ALU.add),
                        reads=[("ps", bank), "h"], writes=["h"])
            if DBG_H1 == 1:
                store_h1(j)
            B.ffn_ple(0, h_ext, TE, HALO, pTb, after_ffn=(lambda j=j: store_h1(j)) if DBG_H1 == 2 else None)
            if DBG_H1 == 3:
                store_h1(j)
            B.rms_stats(h_ext, [(HALO, T)], B.sqb, B.rstd, "h")
            B.normalize(h_ext, B.hn, B.rstd, TE, HALO, T, "h", ("rstd", HALO))
            qkb = zb
            for g in range(4):
                sq_, rq = B.ws_next(B.wspecs["wqk"], g)
                for pi in range(4):
                    ci = g * 4 + pi
                    bank = B.next_bank()
                    B.mm_chain(bank, T, [(sq_[:, (pi * 8 + k) * 128:(pi * 8 + k + 1) * 128],
                                          B.hn[:, k * TE + HALO:k * TE + TE]) for k in range(8)],
                               reads=[rq] + hn_main)
                    sc, scr = B.next_sc()
                    S.add("act", lambda e, sc=sc, bank=bank: e.activation(sc[:, 0:T], B.ps[bank][:, 0:T], AF.Copy),
                          reads=[("ps", bank)], writes=[scr])
                    sq2 = ysq[:, (ci % 8) * T:(ci % 8 + 1) * T]
                    S.add("act", lambda e, sq2=sq2, bank=bank: e.activation(sq2, B.ps[bank][:, 0:T], AF.Square),
                          reads=[("ps", bank)], writes=[("ysq", ci % 8)])
                    b2 = B.next_bank()
                    B.mm_chain(b2, T, [(B.blk64b, sq2)], reads=["constb", ("ysq", ci % 8)])
                    sc2, scr2 = B.next_sc()
                    S.add("act", lambda e, sc2=sc2, b2=b2: e.activation(
                        sc2[:, 0:T], B.ps[b2][:, 0:T], AF.Sqrt, bias=B.cc("eps"), scale=1.0),
                        reads=[("ps", b2)], writes=[scr2])
                    S.add("dve", lambda e, sc2=sc2: e.reciprocal(sc2[:, 0:T], sc2[:, 0:T]),
                          reads=[scr2], writes=[scr2])
                    gcol = gq8 if ci < 8 else B.cc("k_g")
                    qo = qkb[:, (ci % 8) * T:(ci % 8 + 1) * T]
                    S.add("dve", lambda e, qo=qo, sc=sc, sc2=sc2, gcol=gcol: e.scalar_tensor_tensor(
                        qo, sc[:, 0:T], gcol, sc2[:, 0:T], ALU.mult, ALU.mult),
                        reads=[scr, scr2, "gq8", "cst"], writes=[("z", ci % 8)])
                if g == 1 or g == 3:
                    dst = q_out if g == 1 else k_out
                    nm = "qo" if g == 1 else "ko"
                    S.add("pool", lambda e, dst=dst, j=j: e.dma_start(
                        out=dst[j].rearrange("k p t -> p k t"), in_=qkb.rearrange("p (k t) -> p k t", t=T)),
                        reads=[("z", k) for k in range(8)], writes=[(nm, j)], dma="qko")
            for cg in range(2):
                sv, rv = B.ws_next(B.wspecs["wv"], cg)
                for tb in range(4):
                    bank = B.next_bank()
                    B.mm_chain(bank, 512, [(B.hn[:, k * TE + HALO + tb * 128:k * TE + HALO + (tb + 1) * 128],
                                            sv[:, k * 512:(k + 1) * 512]) for k in range(8)],
                               reads=[rv] + hn_main)
                    vo = vt[:, tb * NH * 129:(tb + 1) * NH * 129].rearrange("p (h c) -> p h c", c=129)[:, cg * 4:(cg + 1) * 4, 0:128]
                    S.add("dve", lambda e, vo=vo, bank=bank: e.tensor_copy(
                        vo, B.ps[bank][:, 0:512].rearrange("p (h c) -> p h c", c=128)),
                        reads=[("ps", bank)], writes=["vt"])
            S.add("pool", lambda e, j=j: e.dma_start(
                out=v_out[j].rearrange("b p c -> p b c"), in_=vt.rearrange("p (b c) -> p b c", c=NH * 129)),
                reads=["vt"], writes=[("vo", j)], dma="vo")


def prog2_io(B, NT, fused, io1=None):
    nc = B.nc
    io = {}
    sfx = "1" if fused else ""
    if fused:
        io["h1"] = io1["h1"]
        io["qT"] = io1["qT"]
        io["kp_list"] = [nc.dram_tensor(f"kTp{j}", [2 * 8 * 128, T], BF16).ap() for j in range(NT)]
        io["vp_list"] = [nc.dram_tensor(f"vp{j}", [2 * 4 * 128, NH * 129], BF16).ap() for j in range(NT)]
        kpv = [a.rearrange("(r h p) t -> r h p t", r=2, p=128) for a in io["kp_list"]]
        vpv = [a.rearrange("(r b p) c -> r b p c", r=2, p=128) for a in io["vp_list"]]
        io["kin"] = lambda r_, j_, h: kpv[j_][r_, h, :, :]
        io["vin"] = lambda r_, j_, h: vpv[j_][r_, :, :, h * 129:(h + 1) * 129]
        io["kTp"] = None
        io["vp"] = None
    else:
        io["h1"] = B.dram_in("h1", [NT, 8, 128, T])
        io["qT"] = B.dram_in("qT", [NT, 8, 128, T], BF16)
        io["kTp"] = B.dram_in("kTp", [2, NT, 8, 128, T], BF16)
        io["vp"] = B.dram_in("vp", [2, NT, 4, 128, NH * 129], BF16)
        io["kin"] = lambda r_, j_, h: io["kTp"][r_, j_, h, :, :]
        io["vin"] = lambda r_, j_, h: io["vp"][r_, j_, :, :, h * 129:(h + 1) * 129]
    io["pT"] = B.dram_in("pT1" if fused else "pT", [NT, 2, 128, T])
    io["w_o"] = B.dram_in("attn_w_o", [D, D])
    io["w_gate"] = B.dram_in("ffn_w_gate" + sfx, [D, DFF])
    io["w_up"] = B.dram_in("ffn_w_up" + sfx, [D, DFF])
    io["w_down"] = B.dram_in("ffn_w_down" + sfx, [DFF, D])
    io["w_pg"] = B.dram_in("ple_w_gate" + sfx, [D, D])
    io["w_pp"] = B.dram_in("ple_w_proj" + sfx, [256, D])
    io["out"] = B.dram_out("outT", [NT, 8, 128, T])
    io["vz_t"] = nc.dram_tensor("vz", [8, 384], F32)
    return io


def prog2_specs(io):
    return [
        WSpec("wo", io["w_o"], 0, 8, 0, D, False, "wo_scale", 512),
        WSpec("wg1", io["w_gate"], 0, 8, 0, DFF, False, "ffn_ng1", 512),
        WSpec("wu1", io["w_up"], 0, 8, 0, DFF, False, "ffn_ng1", 512),
        WSpec("wd1", io["w_down"], 0, 22, 0, D, False, None, 128),
        WSpec("pg1", io["w_pg"], 0, 8, 0, D, False, "ple_ng1", 512),
        WSpec("pp1", io["w_pp"], 0, 2, 0, D, False, None, 1024),
    ]


def p2_sizes(NT):
    NKB = 8 * NT
    SEQ = NKB * 128
    f = max(2 * 2048 + 9 * 512 + 512 + 512 + 256 + 64, 8 * T + 4 * T + T + 2 * T + 64)
    b = max(2 * SEQ + 2 * (NKB * 129 + 16) + 2 * NT * T + 2 * 9 * 512 + 6 * 512 + 512 + 2 * 512 + 64,
            NSLOT * SLOT + 8 * T + 8 * T + 22 * T + 2 * T + 8 * T + 64)
    return f, b


def shared_setup(B, es, nf32, nbf, Lc):
    nc, S = B.nc, B.S
    B.fa = Arena(es.enter_context(nc.sbuf_tensor("fa", [128, nf32], F32)), nf32)
    B.ba = Arena(es.enter_context(nc.sbuf_tensor("ba", [128, nbf], BF16)), nbf)
    fa, ba = B.fa, B.ba
    B.cst = fa.alloc(B.L.n)[:, 0:B.L.n]
    B.constb = ba.alloc(512)
    B.onesb = B.constb[:, 0:128]
    B.identb = B.constb[:, 128:256]
    B.blk64b = B.constb[:, 256:384]
    S.add("sp", lambda e: e.dma_start(out=B.cst[:, 0:Lc], in_=B.cst_in), writes=["cst"], dma="cst")
    S.add("pool", lambda e: e.memset(B.onesb, 1.0 / D), writes=["constb"])
    S.add("dve", lambda e: e.tensor_copy(B.identb, B.cc("ident", 0, 128)), reads=["cst"], writes=["constb"])
    S.add("dve", lambda e: e.tensor_copy(B.blk64b, B.cc("blk64", 0, 128)), reads=["cst"], writes=["constb"])
    S.add("dve", lambda e: e.tensor_scalar_mul(B.cc("wo_scale"), B.cc("sub_g"),
                                               math.sqrt(128.0) * (1.0 - LAMBDA_INIT)),
          reads=["cst"], writes=["cst"])
    B.ps = [es.enter_context(nc.psum_tensor(f"ps{i}", [128, 512], F32)) for i in range(7)]
    B.psT = es.enter_context(nc.psum_tensor("psT", [128, 1024], BF16))
    B.banks = list(range(7))
    B.small = fa.alloc(64)


def build_prog2(NT):
    B = Builder(NT, 2)
    nc, S = B.nc, B.S
    B.cst_in = B.dram_in("cst", [128, B.L.n])
    Lc = B.L.n
    B.L.add("wo_scale", 1)
    io = prog2_io(B, NT, False)
    specs = prog2_specs(io)
    B.wspecs = {w.name: w for w in specs}
    B.ws_init(([("wo", 0), ("wo", 1)] + B.ffn_plan(1)) * NT)
    f2, b2 = p2_sizes(NT)
    with ExitStack() as es:
        shared_setup(B, es, B.L.n + 16 + 64 + max(8192, f2), 512 + max(8192, b2), Lc)
        fa, ba = B.fa, B.ba
        mark_f, mark_b = fa.off, ba.off
        st32 = [fa.alloc(4096), fa.alloc(4096)]
        stb = [ba.alloc(4096), ba.alloc(4096)]
        B.preconvert(specs, st32, stb)
        fa.off, ba.off = mark_f, mark_b
        prog2_body(B, io, NT)
        S.barrier()
        S.emit()
    return nc


def build_fused(NT, NB):
    B = Builder(NT, 0)
    nc, S = B.nc, B.S
    B.cst_in = B.dram_in("cst", [128, B.L.n])
    Lc = B.L.n
    B.L.add("wo_scale", 1)
    io1 = prog1_io(B, NT, True)
    io2 = prog2_io(B, NT, True, io1)
    specs = prog1_specs(io1) + prog2_specs(io2)
    B.wspecs = {w.name: w for w in specs}
    f2, b2 = p2_sizes(NT)
    with ExitStack() as es:
        shared_setup(B, es, B.L.n + 16 + 64 + max(8192, f2, P1_F32), 512 + max(8192, b2, P1_BF), Lc)
        fa, ba = B.fa, B.ba
        mark_f, mark_b = fa.off, ba.off
        st32 = [fa.alloc(4096), fa.alloc(4096)]
        stb = [ba.alloc(4096), ba.alloc(4096)]
        B.preconvert(specs, st32, stb)
        fa.off, ba.off = mark_f, mark_b
        B.ws_init(prog1_plan(B) * NT)
        prog1_body(B, io1, NT)
        S.barrier()
        fa.off, ba.off = mark_f, mark_b
        groups = [[2 * b, 2 * b + 1] for b in range(NB)]
        for j in range(NT):
            S.add("pool", lambda e, j=j: e.collective_compute(
                "AllGather", ALU.bypass, replica_groups=groups,
                ins=[io1["kT_list"][j]], outs=[io2["kp_list"][j]]),
                reads=[("ko", j)], writes=[("kpair", j)], dma=("cck", j), dma_inc=1)
            S.add("pool", None, reads=[("kpair", j)])
            S.add("pool", lambda e, j=j: e.collective_compute(
                "AllGather", ALU.bypass, replica_groups=groups,
                ins=[io1["v_list"][j]], outs=[io2["vp_list"][j]]),
                reads=[("vo", j)], writes=[("vpair", j)], dma=("ccv", j), dma_inc=1)
            S.add("pool", None, reads=[("vpair", j)])
        B.ws_init(([("wo", 0), ("wo", 1)] + B.ffn_plan(1)) * NT)
        prog2_body(B, io2, NT)
        S.barrier()
        S.emit()
    return nc


def prog2_body(B, io, NT):
    nc, S = B.nc, B.S
    fa, ba = B.fa, B.ba
    NKB = 8 * NT
    SEQ = NKB * 128
    h1_in, q_in, k_in, v_in, pT_in, out = io["h1"], io["qT"], io["kTp"], io["vp"], io["pT"], io["out"]
    vz_t = io["vz_t"]
    vz = vz_t.ap()
    small, psT = B.small, B.psT
    if True:
        on_scr = nc.dram_tensor("on_scr", [NT, NH, 128, T], BF16).ap()
        mark_f2, mark_b2 = fa.off, ba.off
        E_rev = fa.alloc(8 * 256)
        E_sb = fa.alloc(8 * 256)
        Mt = fa.alloc(9 * 512)
        otmp = fa.alloc(4 * 128)
        osqs = fa.alloc(512)
        ew8 = fa.alloc(256)
        kbuf = [ba.alloc(SEQ) for _ in range(2)]
        vbuf = [ba.alloc(NKB * 129) for _ in range(2)]
        qbuf = [ba.alloc(NT * T) for _ in range(2)]
        Mh = [ba.alloc(9 * 512) for _ in range(2)]
        Pb = [[ba.alloc(512) for _ in range(3)] for _ in range(2)]
        onb = ba.alloc(4 * 128)
        onb2 = [ba.alloc(512) for _ in range(2)]
        zer = ba.alloc(16)
        S.barrier()
        lam = small[:, 0:1]
        neglam = small[:, 1:2]
        s12 = small[:, 2:4]
        cb = small[:, 8:16]
        nb8 = small[:, 16:17]
        lt = otmp[:, 0:128]
        for i_, (a_, b_) in enumerate((("lq1", "lk1"), ("lq2", "lk2"))):
            S.add("dve", lambda e, a_=a_, b_=b_, i_=i_: e.tensor_tensor(
                lt[:, i_ * 64:(i_ + 1) * 64], B.cc(a_, 0, 64), B.cc(b_, 0, 64), ALU.mult),
                reads=["cst"], writes=[("lt", i_)])
            S.add("dve", lambda e, i_=i_: e.reduce_sum(s12[:, i_:i_ + 1], lt[:, i_ * 64:(i_ + 1) * 64], AX.X),
                  reads=[("lt", i_)], writes=[("s12", i_)])
        S.add("act", lambda e: e.activation(s12, s12, AF.Exp), reads=[("s12", 0), ("s12", 1)], writes=["e12"])
        S.add("dve", lambda e: e.tensor_tensor(lam, s12[:, 0:1], s12[:, 1:2], ALU.subtract), reads=["e12"], writes=["lam"])
        S.add("dve", lambda e: e.tensor_scalar(neglam, lam, LAMBDA_INIT, -1.0, ALU.add, ALU.mult),
              reads=["lam"], writes=["neglam"])
        relb = B.cc("relb", 0, 8, 0, 32)
        S.add("pe", lambda e: e.matmul(B.ps[0][0:8, 0:256], relb, B.cc("ohd", 0, 256, 0, 32), start=True, stop=True),
              reads=["cst"], writes=[("ps", 0)])
        S.add("pe", lambda e: e.matmul(B.ps[1][:, 0:8], B.cc("sel31", 0, 128, 0, 32), relb, start=True, stop=True),
              reads=["cst"], writes=[("ps", 1)])
        S.add("dve", lambda e: e.tensor_copy(cb, B.ps[1][:, 0:8]), reads=[("ps", 1)], writes=["cb"])
        S.add("dve", lambda e: e.tensor_scalar_mul(nb8[0:8, :], B.ps[0][0:8, 255:256], -1.0),
              reads=[("ps", 0)], writes=["nb8"])
        S.add("act", lambda e: e.activation(ew8[0:8, :], B.ps[0][0:8, 0:256], AF.Exp, bias=nb8[0:8, :], scale=1.0),
              reads=[("ps", 0), "nb8"], writes=["ew8"])
        S.add("pool", lambda e: e.memset(E_rev[0:8, 0:128], 0.0), writes=["erz"])
        S.add("pool", lambda e: e.dma_start(out=vz[:, 0:127], in_=E_rev[0:8, 0:127]), reads=["erz"],
              writes=["vz0"], dma="vz0")
        S.add("pool", lambda e: e.dma_start(out=vz[:, 127:383], in_=ew8[0:8, :]), reads=["ew8"],
              writes=["vz1"], dma="vz1")
        hank = bass.AP(tensor=vz_t, offset=0, ap=[[1, 128], [384, 8], [1, 256]])
        S.add("sp", lambda e: e.dma_start(out=E_rev.rearrange("p (h c) -> p h c", c=256), in_=hank),
              reads=["vz0", "vz1"], writes=["erev", "erz"], dma="erev")
        for hp in range(4):
            S.add("pe", lambda e, hp=hp: e.matmul(B.ps[hp][:, 0:512], B.cc("antiI", 0, 128),
                                                  E_rev[:, hp * 512:(hp + 1) * 512], start=True, stop=True),
                  reads=["cst", "erev", "ew8", "cb", "nb8"], writes=[("ps", hp)])
            S.add("act", lambda e, hp=hp: e.activation(E_sb[:, hp * 512:(hp + 1) * 512], B.ps[hp][:, 0:512], AF.Copy),
                  reads=[("ps", hp)], writes=["esb"])
        S.add("pool", lambda e: e.memset(otmp[:, 0:128], 1.0), reads=[("lt", 0), ("lt", 1)],
              writes=[("lt", 0), ("lt", 1)])
        for s in range(9):
            for qs in range(4):
                o = Mt[:, s * 512 + qs * 128:s * 512 + (qs + 1) * 128]
                S.add("pool", lambda e, o=o, s=s, qs=qs: e.tensor_scalar_mul(o, otmp[:, 0:128], B.cc("bgval", s * 4 + qs)),
                      reads=[("lt", 0), "cst"], writes=["Mt"])
        S.add("pool", lambda e: e.memset(zer, 0.0), writes=["zer"])

        def load_head(h):
            hb = h % 2
            for r_ in range(2):
                for j_ in range(NT):
                    S.add("sp", lambda e, r_=r_, h=h, hb=hb, j_=j_: e.dma_start(
                        out=kbuf[hb].rearrange("p (j r t) -> p j r t", r=2, t=T)[:, j_, r_, :],
                        in_=io["kin"](r_, j_, h)),
                        reads=[("kpair", j_)], writes=[("kbuf", hb)], dma=("kbuf", hb))
                    S.add("sp", lambda e, r_=r_, h=h, hb=hb, j_=j_: e.dma_start(
                        out=vbuf[hb].rearrange("p (j r b c) -> p j r b c", r=2, b=4, c=129)[:, j_, r_, :, :],
                        in_=io["vin"](r_, j_, h).rearrange("b p c -> p b c")),
                        reads=[("vpair", j_)], writes=[("vbuf", hb)], dma=("vbuf", hb))
            S.add("sp", lambda e, h=h, hb=hb: e.dma_start(
                out=qbuf[hb].rearrange("p (j t) -> p j t", t=T),
                in_=q_in[:, h, :, :].rearrange("j p t -> p j t")),
                writes=[("qbuf", hb)], dma=("qbuf", hb))
            S.add("pool", lambda e, hb=hb: e.tensor_copy(Mh[hb], Mt), reads=["Mt"], writes=[("Mh", hb)])
            for qs in range(4):
                for (s, cname, ecol) in ((qs, "cnear", 128), (qs + 4, "cnear", 128),
                                         (qs + 1, "cdiag", 0), (qs + 5, "cdiag", 0)):
                    o = Mh[hb][:, s * 512 + qs * 128:s * 512 + (qs + 1) * 128]
                    src = E_sb[:, h * 256 + ecol:h * 256 + ecol + 128]
                    mt = Mt[:, s * 512 + qs * 128:s * 512 + (qs + 1) * 128]
                    S.add("dve", lambda e, o=o, src=src, mt=mt, s=s, qs=qs, cname=cname: e.scalar_tensor_tensor(
                        o, src, B.cc(cname, s * 4 + qs), mt, ALU.mult, ALU.add),
                        reads=["esb", "Mt", "cst"], writes=[("Mh", hb)])

        def acc_ap(c, qs):
            a = c * 4 + qs
            return B.ps[4 + a // 3][:, (a % 3) * 129:(a % 3) * 129 + 129]

        pr = 0
        load_head(0)
        for h in range(NH):
            hb = h % 2
            if h + 1 < NH:
                load_head(h + 1)
            for j in range(NT):
                nkb = 8 * j + 8
                def stage_a(kb, pbuf, j=j, h=h, hb=hb):
                    sbuf_ = kb % 2
                    for c in range(2):
                        bank = c * 2 + sbuf_
                        S.add("pe", lambda e, c=c, bank=bank, kb=kb: e.matmul(
                            B.ps[bank][:, 0:512], kbuf[hb][c * 64:(c + 1) * 64, kb * 128:(kb + 1) * 128],
                            qbuf[hb][c * 64:(c + 1) * 64, j * T:(j + 1) * T], start=True, stop=True),
                            reads=[("kbuf", hb), ("qbuf", hb)], writes=[("ps", bank)])
                        P = Pb[c][pbuf]
                        S.add("act", lambda e, P=P, bank=bank: e.activation(
                            P, B.ps[bank][:, 0:512], AF.Exp, bias=cb[:, h:h + 1], scale=1.0),
                            reads=[("ps", bank), "cb"], writes=[("P", c, pbuf)])
                        s = kb - (8 * j - 1)
                        if s >= 0:
                            S.add("dve", lambda e, P=P, s=s: e.tensor_tensor(
                                P, P, Mh[hb][:, s * 512:(s + 1) * 512], ALU.mult),
                                reads=[("P", c, pbuf), ("Mh", hb)], writes=[("P", c, pbuf)])

                def stage_b(kb, pbuf, hb=hb, nkb=nkb):
                    for c in range(2):
                        P = Pb[c][pbuf]

                        def pv(e, c=c, P=P, kb=kb):
                            bi = None
                            for qs in range(4):
                                a = c * 4 + qs
                                bi = e.matmul(acc_ap(c, qs), P[:, qs * 128:(qs + 1) * 128],
                                              vbuf[hb][:, kb * 129:(kb + 1) * 129],
                                              start=(kb == 0 and a in (0, 3, 6)), stop=(kb == nkb - 1),
                                              skip_group_check=True)
                            return bi
                        S.add("pe", pv, reads=[("P", c, pbuf), ("vbuf", hb)], writes=["acc"])

                pbs = [(pr + kb) % 3 for kb in range(nkb)]
                pr += nkb
                stage_a(0, pbs[0])
                for kb in range(nkb):
                    if kb + 1 < nkb:
                        stage_a(kb + 1, pbs[kb + 1])
                    stage_b(kb, pbs[kb])
                ops = [[] for _ in range(4)]
                for qs in range(4):
                    def dadd(*a_, qs=qs, **k_):
                        ops[qs].append((a_, k_))
                    osq = osqs[:, qs * 128:(qs + 1) * 128]
                    a0 = acc_ap(0, qs)
                    a1 = acc_ap(1, qs)
                    rr = small[:, 20 + qs * 4:24 + qs * 4]
                    dadd("dve", lambda e, rr=rr, a0=a0: e.reciprocal(rr[:, 0:1], a0[:, 128:129]),
                          reads=["acc"], writes=[("rr", qs)])
                    dadd("dve", lambda e, rr=rr, a1=a1: e.reciprocal(rr[:, 1:2], a1[:, 128:129]),
                          reads=["acc"], writes=[("rr", qs)])
                    dadd("dve", lambda e, rr=rr: e.tensor_tensor(rr[:, 2:3], rr[:, 1:2], neglam, ALU.mult),
                          reads=[("rr", qs), "neglam"], writes=[("rr", qs)])
                    ot = otmp[:, qs * 128:(qs + 1) * 128]
                    dadd("dve", lambda e, ot=ot, a1=a1, rr=rr: e.tensor_scalar_mul(ot, a1[:, 0:128], rr[:, 2:3]),
                          reads=["acc", ("rr", qs)], writes=[("ot", qs)])
                    dadd("dve", lambda e, ot=ot, a0=a0, rr=rr: e.scalar_tensor_tensor(
                        ot, a0[:, 0:128], rr[:, 0:1], ot, ALU.mult, ALU.add),
                        reads=["acc", ("ot", qs)], writes=[("ot", qs)])
                    dadd("act", lambda e, ot=ot, osq=osq: e.activation(osq, ot, AF.Square), reads=[("ot", qs)], writes=[("osq", qs)])
                    dadd("dve", lambda e, rr=rr, osq=osq: e.reduce_sum(rr[:, 3:4], osq, AX.X), reads=[("osq", qs)], writes=[("rr", qs)])
                    dadd("act", lambda e, rr=rr: e.activation(rr[:, 3:4], rr[:, 3:4], AF.Sqrt, bias=B.cc("eps128"), scale=1.0),
                          reads=[("rr", qs)], writes=[("rr", qs)])
                    dadd("dve", lambda e, rr=rr: e.reciprocal(rr[:, 3:4], rr[:, 3:4]), reads=[("rr", qs)], writes=[("rr", qs)])
                    ob = onb[:, qs * 128:(qs + 1) * 128]
                    dadd("pool", lambda e, ob=ob, ot=ot, rr=rr: e.tensor_scalar_mul(ob, ot, rr[:, 3:4]),
                          reads=[("ot", qs), ("rr", qs)], writes=[("onb", qs)])
                    dadd("pe", lambda e, ob=ob, qs=qs: e.transpose(psT[:, qs * 128:(qs + 1) * 128], ob, B.identb),
                          reads=[("onb", qs), "constb"], writes=["psT"])
                for i_ in range(max(len(o) for o in ops)):
                    for qs in range(4):
                        if i_ < len(ops[qs]):
                            a_, k_ = ops[qs][i_]
                            S.add(*a_, **k_)
                ob2 = (h * NT + j) % 2
                S.add("act", lambda e, ob2=ob2: e.activation(onb2[ob2], psT[:, 0:512], AF.Copy),
                      reads=["psT"], writes=[("onb2", ob2)])
                S.add("pool", lambda e, ob2=ob2, h=h, j=j: e.dma_start(out=on_scr[j, h], in_=onb2[ob2]),
                      reads=[("onb2", ob2)], writes=[("onscr", j)], dma=("onb2", ob2))
        S.barrier()
        fa.off, ba.off = mark_f2, mark_b2
        B.wslots = [ba.alloc(SLOT) for _ in range(NSLOT)]
        on_t = ba.alloc(8 * T)
        hb_ = fa.alloc(8 * T)
        B.scs = [fa.alloc(T) for _ in range(4)]
        B.rstd = fa.alloc(T)
        p32 = fa.alloc(2 * T)
        B.sqb = ba.alloc(8 * T)
        B.hn = ba.alloc(8 * T)
        B.abuf = ba.alloc(22 * T)
        pTb = ba.alloc(2 * T)
        for j in range(NT):
            S.add("sp", lambda e, j=j: e.dma_start(out=hb_.rearrange("p (k t) -> p k t", t=T),
                                                   in_=h1_in[j].rearrange("k p t -> p k t")),
                  writes=["h"], dma="h")
            B.load_pT(pT_in, j, p32, pTb)
            S.add("sp", lambda e, j=j: e.dma_start(out=on_t.rearrange("p (k t) -> p k t", t=T),
                                                   in_=on_scr[j].rearrange("k p t -> p k t")),
                  reads=[("onscr", j)], writes=["on_t"], dma="on_t")
            on_res = ["on_t"]
            for g in range(2):
                so, ro = B.ws_next(B.wspecs["wo"], g)
                for pi in range(4):
                    ci = g * 4 + pi
                    bank = B.next_bank()
                    B.mm_chain(bank, T, [(so[:, (pi * 8 + k) * 128:(pi * 8 + k + 1) * 128],
                                          on_t[:, k * T:(k + 1) * T]) for k in range(8)],
                               reads=[ro] + on_res)
                    hk = hb_[:, ci * T:(ci + 1) * T]
                    S.add("dve", lambda e, hk=hk, bank=bank: e.tensor_tensor(hk, B.ps[bank][:, 0:T], hk, ALU.add),
                          reads=[("ps", bank), "h"], writes=["h"])
            B.ffn_ple(1, hb_, T, 0, pTb)
            S.add("pool", lambda e, j=j: e.dma_start(out=out[j].rearrange("k p t -> p k t"),
                                                     in_=hb_.rearrange("p (k t) -> p k t", t=T)),
                  reads=["h"], writes=[("out", j)], dma="outst")


_PROG_CACHE = {}


def _get_prog(which, NT):
    key = (which, NT)
    if key not in _PROG_CACHE:
        _PROG_CACHE[key] = build_prog1(NT) if which == 1 else build_prog2(NT)
    return _PROG_CACHE[key]


def run_model(inp, NB, SEQ):
    NT = SEQ // T // 2
    ncores = 2 * NB
    x = np.asarray(inp["x"], np.float32)
    p = np.asarray(inp["p"], np.float32)
    wnames1 = ["conv_w_pw1", "conv_w_pw2", "attn_w_qkv"]
    in1 = []
    in2 = []
    for c in range(ncores):
        b, r = c // 2, c % 2
        xh = np.zeros((NT, 8, 128, TE), np.float32)
        pT0 = np.zeros((NT, 2, 128, T), np.float32)
        pT1 = np.zeros((NT, 2, 128, T), np.float32)
        for j in range(NT):
            t0 = (2 * j + r) * T
            lo = t0 - HALO
            seg = np.zeros((TE, D), np.float32)
            if lo >= 0:
                seg[:] = x[b, lo:t0 + T]
            else:
                seg[HALO:] = x[b, t0:t0 + T]
            xh[j] = seg.T.reshape(8, 128, TE)
            pT0[j] = p[0, b, t0:t0 + T].T.reshape(2, 128, T)
            pT1[j] = p[1, b, t0:t0 + T].T.reshape(2, 128, T)
        cst = build_consts(inp, NT, r)
        m1 = {"cst": cst, "xh": xh, "pT": pT0}
        for nm in wnames1:
            m1[nm] = np.ascontiguousarray(np.asarray(inp[nm], np.float32)[0])
        for nm in ["ffn_w_gate", "ffn_w_up", "ffn_w_down", "ple_w_gate", "ple_w_proj"]:
            m1[nm] = np.ascontiguousarray(np.asarray(inp[nm], np.float32)[0])
        in1.append(m1)
        m2 = {"cst": cst, "pT": pT1, "attn_w_o": np.ascontiguousarray(np.asarray(inp["attn_w_o"], np.float32)[0])}
        for nm in ["ffn_w_gate", "ffn_w_up", "ffn_w_down", "ple_w_gate", "ple_w_proj"]:
            m2[nm] = np.ascontiguousarray(np.asarray(inp[nm], np.float32)[1])
        in2.append(m2)
    nc1 = _get_prog(1, NT)
    res1 = run_bass_kernel_spmd(nc1, in1, core_ids=list(range(ncores))).results
    for c in range(ncores):
        b = c // 2
        in2[c]["h1"] = res1[c]["h1"]
        in2[c]["qT"] = res1[c]["qT"]
        in2[c]["kTp"] = np.stack([res1[2 * b]["kT"], res1[2 * b + 1]["kT"]])
        in2[c]["vp"] = np.stack([res1[2 * b]["v"], res1[2 * b + 1]["v"]])
    nc2 = _get_prog(2, NT)
    res2 = run_bass_kernel_spmd(nc2, in2, core_ids=list(range(ncores))).results
    out = np.zeros((NB, SEQ, D), np.float32)
    for c in range(ncores):
        b, r = c // 2, c % 2
        o = np.asarray(res2[c]["outT"], np.float32)
        for j in range(NT):
            t0 = (2 * j + r) * T
            out[b, t0:t0 + T] = o[j].reshape(D, T).T
    return out, res1, res2


def run_fused(inp, NB, SEQ):
    NT = SEQ // T // 2
    ncores = 2 * NB
    x = np.asarray(inp["x"], np.float32)
    p = np.asarray(inp["p"], np.float32)
    shared = {}
    for nm in ["conv_w_pw1", "conv_w_pw2", "attn_w_qkv", "attn_w_o"]:
        shared[nm] = np.ascontiguousarray(np.asarray(inp[nm], np.float32)[0])
    for nm in ["ffn_w_gate", "ffn_w_up", "ffn_w_down", "ple_w_gate", "ple_w_proj"]:
        for l in range(2):
            shared[nm + str(l)] = np.ascontiguousarray(np.asarray(inp[nm], np.float32)[l])
    in_maps = []
    for c in range(ncores):
        b, r = c // 2, c % 2
        xh = np.zeros((NT, 8, 128, TE), np.float32)
        pT0 = np.zeros((NT, 2, 128, T), np.float32)
        pT1 = np.zeros((NT, 2, 128, T), np.float32)
        for j in range(NT):
            t0 = (2 * j + r) * T
            lo = t0 - HALO
            seg = np.zeros((TE, D), np.float32)
            if lo >= 0:
                seg[:] = x[b, lo:t0 + T]
            else:
                seg[HALO:] = x[b, t0:t0 + T]
            xh[j] = seg.T.reshape(8, 128, TE)
            pT0[j] = p[0, b, t0:t0 + T].T.reshape(2, 128, T)
            pT1[j] = p[1, b, t0:t0 + T].T.reshape(2, 128, T)
        m = {"cst": build_consts(inp, NT, r), "xh": xh, "pT0": pT0, "pT1": pT1}
        m.update(shared)
        in_maps.append(m)
    key = ("fused", NT, NB)
    if key not in _PROG_CACHE:
        _PROG_CACHE[key] = build_fused(NT, NB)
    res = run_bass_kernel_spmd(_PROG_CACHE[key], in_maps, core_ids=list(range(ncores))).results
    out = np.zeros((NB, SEQ, D), np.float32)
    for c in range(ncores):
        b, r = c // 2, c % 2
        o = np.asarray(res[c]["outT"], np.float32)
        for j in range(NT):
            t0 = (2 * j + r) * T
            out[b, t0:t0 + T] = o[j].reshape(D, T).T
    return out


def kernel(**inputs):
    return run_fused(inputs, 4, 8192)
```
